# Optimizing a Trainium2 kernel written in Bass

```python
import jax, jax.numpy as jnp
from jax import lax
import numpy as np

D_MODEL = 2048
BATCH = 8
SEQ = 4096
DEPTH = 1

PLE_DIM = 256
ATTN_WIDTH = D_MODEL // 2
CONV_WIDTH = D_MODEL - ATTN_WIDTH
HEAD_DIM = 64
N_Q_HEADS = ATTN_WIDTH // HEAD_DIM
N_KV_HEADS = 4
GQA_GROUP = N_Q_HEADS // N_KV_HEADS
KV_WIDTH = N_KV_HEADS * HEAD_DIM
WINDOW = 128
BLOCK = 128
ROT_DIM = HEAD_DIM // 4
ROPE_THETA = 500000.0
CONV_K = 3
CONV_GROUPS = 16
EPS = 1e-6
NEG_INF = -1e30

SEG_WIDTHS = (ATTN_WIDTH, KV_WIDTH, KV_WIDTH, ATTN_WIDTH,
              CONV_WIDTH, CONV_WIDTH, CONV_WIDTH, CONV_WIDTH)
IN_WIDTH = sum(SEG_WIDTHS)
SPLITS = [int(v) for v in np.cumsum(SEG_WIDTHS)[:-1]]

kernel_name = "hymba_swa_sink_shortconv_ple"


def rms_norm(x, gain):
    xf = x.astype(jnp.float32)
    y = xf * lax.rsqrt(jnp.mean(xf * xf, axis=-1, keepdims=True) + EPS)
    return (y * gain.astype(jnp.float32)).astype(x.dtype)


def partial_rope(x, pos):
    half = ROT_DIM // 2
    inv_freq = jnp.power(jnp.float32(ROPE_THETA), -jnp.arange(half, dtype=jnp.float32) * 2.0 / ROT_DIM)
    ang = pos.astype(jnp.float32)[:, None] * inv_freq[None, :]
    cos = jnp.cos(ang)[None, :, None, :]
    sin = jnp.sin(ang)[None, :, None, :]
    xr = x[..., :ROT_DIM].astype(jnp.float32)
    x1, x2 = xr[..., :half], xr[..., half:]
    rot = jnp.concatenate([x1 * cos - x2 * sin, x2 * cos + x1 * sin], axis=-1).astype(x.dtype)
    return jnp.concatenate([rot, x[..., ROT_DIM:]], axis=-1)


def sliding_window_attention(q, k, v, sinks):
    b, s = q.shape[0], q.shape[1]
    nb = s // BLOCK
    qb = q.reshape(b, nb, BLOCK, N_KV_HEADS, GQA_GROUP, HEAD_DIM)

    def band(t):
        tb = t.reshape(b, nb, BLOCK, N_KV_HEADS, HEAD_DIM)
        prev = jnp.pad(tb, ((0, 0), (1, 0), (0, 0), (0, 0), (0, 0)))[:, :-1]
        return jnp.concatenate([prev, tb], axis=2)

    kb, vb = band(k), band(v)
    scores = jnp.einsum('bnqhgd,bnkhd->bnhgqk', qb, kb,
                        preferred_element_type=jnp.float32) * (HEAD_DIM ** -0.5)
    q_pos = jnp.arange(nb)[:, None] * BLOCK + jnp.arange(BLOCK)[None, :]
    k_pos = jnp.arange(nb)[:, None] * BLOCK - BLOCK + jnp.arange(2 * BLOCK)[None, :]
    diff = q_pos[:, :, None] - k_pos[:, None, :]
    mask = (diff >= 0) & (diff < WINDOW) & (k_pos[:, None, :] >= 0)
    scores = jnp.where(mask[None, :, None, None], scores, NEG_INF)
    sink = sinks.astype(jnp.float32).reshape(N_KV_HEADS, GQA_GROUP)[None, None, :, :, None, None]
    m = jnp.maximum(jnp.max(scores, axis=-1, keepdims=True), sink)
    e = jnp.exp(scores - m)
    probs = e / (jnp.sum(e, axis=-1, keepdims=True) + jnp.exp(sink - m))
    out = jnp.einsum('bnhgqk,bnkhd->bnqhgd', probs.astype(v.dtype), vb)
    return out.reshape(b, s, N_Q_HEADS * HEAD_DIM)


def short_conv(u, w):
    return lax.conv_general_dilated(
        u, w[:, None, :].astype(u.dtype), window_strides=(1,),
        padding=[(CONV_K - 1, 0)], dimension_numbers=('NWC', 'WIO', 'NWC'),
        feature_group_count=u.shape[-1])


def setup_inputs(seed: int = 0) -> dict:
    key = jax.random.key(seed)
    ks = jax.random.split(key, 16)
    f32 = jnp.float32
    nrm = lambda k, shape, scale: jax.random.normal(k, shape, f32) * scale
    return {
        "x": nrm(ks[0], (BATCH, SEQ, D_MODEL), 1.0),
        "p": nrm(ks[1], (DEPTH, BATCH, SEQ, PLE_DIM), 1.0),
        "norm_gain": 1.0 + nrm(ks[2], (DEPTH, D_MODEL), 0.02),
        "w_in": nrm(ks[3], (DEPTH, D_MODEL, IN_WIDTH), D_MODEL ** -0.5),
        "q_norm_gain": 1.0 + nrm(ks[4], (DEPTH, HEAD_DIM), 0.02),
        "k_norm_gain": 1.0 + nrm(ks[5], (DEPTH, HEAD_DIM), 0.02),
        "attn_sinks": nrm(ks[6], (DEPTH, N_Q_HEADS), 0.5),
        "conv_w": nrm(ks[7], (DEPTH, CONV_K, CONV_WIDTH), CONV_K ** -0.5),
        "w_out": nrm(ks[8], (DEPTH, D_MODEL, D_MODEL), D_MODEL ** -0.5),
        "ple_gate_norm_gain": 1.0 + nrm(ks[9], (DEPTH, D_MODEL), 0.02),
        "w_ple_gate": nrm(ks[10], (DEPTH, D_MODEL, D_MODEL), D_MODEL ** -0.5),
        "b_ple_gate": nrm(ks[11], (DEPTH, D_MODEL), 0.01),
        "w_ple_proj": nrm(ks[12], (DEPTH, PLE_DIM, D_MODEL), PLE_DIM ** -0.5),
        "ple_norm_gain": 1.0 + nrm(ks[13], (DEPTH, D_MODEL), 0.02),
    }


def reference(x, p, norm_gain, w_in, q_norm_gain, k_norm_gain, attn_sinks, conv_w, w_out,
              ple_gate_norm_gain, w_ple_gate, b_ple_gate, w_ple_proj, ple_norm_gain):
    b, s = x.shape[0], x.shape[1]
    pos = jnp.arange(s)
    for i in range(DEPTH):
        h = rms_norm(x, norm_gain[i])
        z = h @ w_in[i]
        q, k, v, g_attn, c_b, c_c, c_h, g_conv = jnp.split(z, SPLITS, axis=-1)

        q = q.reshape(b, s, N_Q_HEADS, HEAD_DIM)
        k = k.reshape(b, s, N_KV_HEADS, HEAD_DIM)
        v = v.reshape(b, s, N_KV_HEADS, HEAD_DIM)
        q = partial_rope(rms_norm(q, q_norm_gain[i]), pos)
        k = partial_rope(rms_norm(k, k_norm_gain[i]), pos)
        y_attn = sliding_window_attention(q, k, v, attn_sinks[i]) * jax.nn.silu(g_attn)

        y_conv = c_b * short_conv(c_c * c_h, conv_w[i]) * jax.nn.silu(g_conv)

        mix = jnp.concatenate([y_attn, y_conv], axis=-1)
        x = x + mix @ w_out[i]

        gate = jax.nn.sigmoid(rms_norm(x, ple_gate_norm_gain[i]) @ w_ple_gate[i] + b_ple_gate[i])
        e = rms_norm(p[i] @ w_ple_proj[i], ple_norm_gain[i])
        x = x + gate * e
    return x
```

```python
import numpy as np
import ml_dtypes
from contextlib import ExitStack
import concourse.bass as bass
import concourse.mybir as mybir
from concourse.bass_utils import run_bass_kernel_spmd

F32 = mybir.dt.float32
BF16 = mybir.dt.bfloat16
ALU = mybir.AluOpType
AF = mybir.ActivationFunctionType
AX = mybir.AxisListType

N_CORES = 8
D = 2048
S = 4096
T = 512
NT = 4
NS = S // T
KC = 16
PLE = 256
IN_W = 6656
EPS = 1e-6
Q0, K0, V0, GA0, CB0, CC0, CH0, GC0 = 0, 1024, 1280, 1536, 2560, 3584, 4608, 5632


class _Op:
    __slots__ = ("q", "fn", "deps", "has_dep", "ms", "is_dma", "sem", "val")


class Sched:
    QUEUES = ("pe", "act", "dve", "pool", "sp")

    def __init__(self):
        self.queues = {q: [] for q in self.QUEUES}
        self.last_w = {}
        self.readers = {}
        self.dma_vals = {}

    def op(self, q, fn, reads=(), writes=(), dma_sem=None, n_dma=1):
        o = _Op()
        o.q = q
        o.fn = fn
        o.has_dep = False
        o.ms = None
        o.is_dma = dma_sem is not None
        o.sem = dma_sem
        if o.is_dma:
            v = self.dma_vals.get(dma_sem, 0) + 16 * n_dma
            self.dma_vals[dma_sem] = v
            o.val = v
        else:
            o.val = None
        def _canon(r):
            return ("bk", r[1]) if (isinstance(r, tuple) and r[0] == "bk") else r
        reads = [_canon(r) for r in reads]
        writes = [_canon(r) for r in writes]
        writes = writes + [r for r in reads if isinstance(r, tuple) and r[0] == "bk"]
        reads = [r for r in reads if not (isinstance(r, tuple) and r[0] == "bk")]
        deps = []
        raw = set()
        for r in reads:
            w = self.last_w.get(r)
            if w is not None:
                deps.append(w)
                raw.add(id(w))
        for r in writes:
            w = self.last_w.get(r)
            if w is not None:
                deps.append(w)
            deps.extend(self.readers.get(r, ()))
        od = []
        seen = set()
        for d in deps:
            if id(d) in seen:
                continue
            seen.add(id(d))
            if d.q == q and not d.is_dma and (id(d) not in raw or q == "pe"):
                continue
            d.has_dep = True
            od.append(d)
        o.deps = od
        for r in reads:
            self.readers.setdefault(r, []).append(o)
        for r in writes:
            self.last_w[r] = o
            self.readers[r] = []
        self.queues[q].append(o)
        return o

    def emit(self, block, qsems, engines):
        for q in self.QUEUES:
            c = 0
            for o in self.queues[q]:
                if o.has_dep and not o.is_dma:
                    c += 1
                    o.ms = c

        def run(q, eng):
            seen = {}
            for o in self.queues[q]:
                need = {}
                for d in o.deps:
                    if d.is_dma:
                        s, v = d.sem, d.val
                    else:
                        s, v = qsems[d.q], d.ms
                    if seen.get(s, 0) >= v:
                        continue
                    if need.get(s, 0) < v:
                        need[s] = v
                for s, v in need.items():
                    eng.wait_ge(s, v)
                    seen[s] = v
                if o.fn is None:
                    continue
                r = o.fn(eng)
                if o.is_dma:
                    for ins in r:
                        ins.then_inc(o.sem, 16)
                elif o.ms is not None:
                    r.then_inc(qsems[q], 1)

        for q in self.QUEUES:
            if not self.queues[q]:
                continue
            dec = getattr(block, engines[q])

            def mk(q=q):
                def _f(eng):
                    run(q, eng)
                return _f
            dec(mk())


def build_program():
    nc = bass.Bass("TRN2", target_bir_lowering=False)
    dram = lambda n, s, d, k: nc.dram_tensor(n, s, d, kind=k).ap()
    x_d = dram("x", [S, D], F32, "ExternalInput")
    p_d = dram("p", [S, PLE], F32, "ExternalInput")
    win_d = dram("w_in", [D, IN_W], F32, "ExternalInput")
    wout_d = dram("w_out", [D, D], F32, "ExternalInput")
    wgate_d = dram("w_gate", [D, D], F32, "ExternalInput")
    wproj_d = dram("w_proj", [PLE, D], F32, "ExternalInput")
    g1T_d = dram("g1T", [128, KC], F32, "ExternalInput")
    g2T_d = dram("g2T", [128, KC], F32, "ExternalInput")
    gqb_d = dram("gqb", [128, 64], F32, "ExternalInput")
    gkb_d = dram("gkb", [128, 64], F32, "ExternalInput")
    gqc_d = dram("gqc", [128, 1], F32, "ExternalInput")
    gkc_d = dram("gkc", [128, 1], F32, "ExternalInput")
    rotm_d = dram("rotm", [128, 1], F32, "ExternalInput")
    sink_d = dram("sinkb", [128, 16], F32, "ExternalInput")
    cw_d = dram("cw", [128, 24], F32, "ExternalInput")
    bias_d = dram("biasb", [128, D], F32, "ExternalInput")
    pg_d = dram("pgb", [128, D], F32, "ExternalInput")
    cos_d = dram("cost", [128, 256], F32, "ExternalInput")
    sin_d = dram("sint", [128, 256], F32, "ExternalInput")
    ident_d = dram("ident", [128, 128], BF16, "ExternalInput")
    mask_d = dram("maskt", [128, 512], BF16, "ExternalInput")
    y_d = dram("y", [S, D], F32, "ExternalOutput")
    winb_d = dram("w_in_bf", [D, IN_W], BF16, "Internal")
    woutb_d = dram("w_out_bf", [D, D], BF16, "Internal")
    wgateb_d = dram("w_gate_bf", [D, D], BF16, "Internal")
    wprojb_d = dram("w_proj_bf", [PLE, D], BF16, "Internal")

    Sd = Sched()
    op = Sd.op
    es = ExitStack()
    with es:
        sb = lambda n, s, d: es.enter_context(nc.sbuf_tensor(n, s, d))
        sem = lambda n: es.enter_context(nc.semaphore(n))
        qs = {q: sem("q_" + q) for q in Sched.QUEUES}
        banks = [es.enter_context(nc.psum_tensor("bank%d" % i, [128, 512], F32)) for i in range(8)]
        bankb = [b.bitcast(BF16) for b in banks]

        xs = [sb("xs%d" % t, [128, D], F32) for t in range(NT)]
        xn = [sb("xn%d" % i, [128, D], BF16) for i in range(2)]
        hT = sb("hT", [128, KC * T], BF16)
        mixT = sb("mixT", [128, KC * T], BF16)
        wsl = [sb("wsl%d" % i, [128, KC * 512], BF16) for i in range(2)]
        qT = [sb("qT%d" % t, [128, 1024], BF16) for t in range(NT)]
        sgl = [sb("sgl%d" % t, [128, 1024], F32) for t in range(NT)]
        kTr = [sb("kTr%d" % i, [128, 512], BF16) for i in range(5)]
        vr = [sb("vr%d" % i, [128, 4 * 65], BF16) for i in range(5)]
        knq = [sb("knq%d" % i, [128, 512], BF16) for i in range(4)]
        Pb = [sb("Pb%d" % i, [128, 512], BF16) for i in range(4)]
        mixtok = [sb("mixtok%d" % i, [128, 1024], BF16) for i in range(2)]
        fsc = [sb("fsc%d" % i, [128, 516], F32) for i in range(8)]
        bch = [sb("bch%d" % i, [128, 512], F32) for i in range(2)]
        pch = [sb("pch%d" % i, [128, 512], F32) for i in range(2)]
        wproj = sb("wproj", [128, 2 * D], BF16)
        pT = sb("pT", [128, 2 * T], BF16)
        psb = [sb("psb%d" % i, [128, PLE], F32) for i in range(2)]
        pbf = [sb("pbf%d" % i, [128, PLE], BF16) for i in range(4)]
        ident = sb("ident_s", [128, 128], BF16)
        maskt = sb("mask_s", [128, 512], BF16)
        g1T = sb("g1T_s", [128, KC], F32)
        g2T = sb("g2T_s", [128, KC], F32)
        gqb = sb("gqb_s", [128, 64], F32)
        gkb = sb("gkb_s", [128, 64], F32)
        qvec = sb("qvec", [128, 1], F32)
        kvec = sb("kvec", [128, 1], F32)
        gqc = sb("gqc_s", [128, 1], F32)
        gkc = sb("gkc_s", [128, 1], F32)
        rotm = sb("rotm_s", [128, 1], F32)
        omr = sb("omr", [128, 1], F32)
        sinkb = sb("sink_s", [128, 16], F32)
        es2 = sb("es2", [128, 16], F32)
        cw = sb("cw_s", [128, 24], F32)
        cost = sb("cos_s", [128, 256], F32)
        sint = sb("sin_s", [128, 256], F32)
        rtab = {}
        for nm in ("qCA", "qSA", "qCB", "qSB", "kCA", "kSA", "kCB", "kSB"):
            rtab[nm] = sb("rt_" + nm, [128, 256], F32)
        neghalf = sb("neghalf", [128, 16], F32)
        small = sb("small", [128, 256], F32)
        ucarry = sb("ucarry", [128, 16], F32)
        rope_t = sb("rope_t", [128, 512], F32)

        _sc = [0]

        def scol(n):
            c = _sc[0]
            _sc[0] += n
            assert _sc[0] <= 256
            return small[:, c:c + n]

        hT3 = hT[:].rearrange("p (c k) -> p c k", c=KC)
        mixT3 = mixT[:].rearrange("p (c k) -> p c k", c=KC)
        w3 = [w[:].rearrange("p (c k) -> p c k", c=KC) for w in wsl]
        wproj3 = wproj[:].rearrange("p (c k) -> p c k", c=2)
        pT3 = pT[:].rearrange("p (c k) -> p c k", c=2)

        s_const = sem("s_const")
        const_list = [(ident, ident_d), (maskt, mask_d), (g1T, g1T_d), (g2T, g2T_d), (gqb, gqb_d),
                      (gkb, gkb_d), (gqc, gqc_d), (gkc, gkc_d), (rotm, rotm_d), (sinkb, sink_d),
                      (cw, cw_d), (cost, cos_d), (sint, sin_d)]

        def ld_const(e):
            return [e.dma_start(out=a[:], in_=b) for a, b in const_list]
        op("sp", ld_const, writes=["const"], dma_sem=s_const, n_dma=len(const_list))

        cast_keys = {}
        cast_list = []
        cast_state = {"n": 0}

        def cast(name, src, dst, c0, c1, r1):
            cast_list.append((name, src, dst, c0, c1, r1))
            cast_keys[(name, c0)] = ("wbf", name, c0)

        def issue_casts(upto):
            while cast_state["n"] <= min(upto, len(cast_list) - 1):
                i = cast_state["n"]
                name, src, dst, c0, c1, r1 = cast_list[i]
                s_ = sem("s_cast_%s_%d" % (name, c0))
                npc = 4 if r1 >= 512 else 1
                rs = r1 // npc

                def f(e, src=src, dst=dst, c0=c0, c1=c1, npc=npc, rs=rs):
                    return [e.dma_start(out=dst[k * rs:(k + 1) * rs, c0:c1], in_=src[k * rs:(k + 1) * rs, c0:c1],
                                        max_dma_last_dim=8192) for k in range(npc)]
                rd = []
                if i >= 2:
                    pn, _, _, pc0, _, _ = cast_list[i - 2]
                    rd = [("wbf", pn, pc0)]
                op("pool", f, reads=rd, writes=[("wbf", name, c0)], dma_sem=s_, n_dma=npc)
                cast_state["n"] = i + 1

        x_sems = [sem("s_x%d" % t) for t in range(NT)]
        y_sems = [sem("s_y%d" % t) for t in range(NT)]
        p_sems = [sem("s_p%d" % i) for i in range(2)]
        w_sems = [sem("s_w%d" % i) for i in range(2)]
        s_wproj = sem("s_wproj")
        bp_sems = [sem("s_bp%d" % i) for i in range(2)]

        xb_sems = [sem("s_xb%d" % t) for t in range(NT)]

        def load_x(s, t):
            r0 = s * T + t * 128

            def fa(e, t=t, r0=r0):
                return [e.dma_start(out=xs[t][:, 0:768], in_=x_d[r0:r0 + 128, 0:768]),
                        e.dma_start(out=xs[t][:, 768:1536], in_=x_d[r0:r0 + 128, 768:1536])]
            op("pool", fa, writes=[("xs", t, 0), ("xs", t, 1), ("xs", t, 2)], dma_sem=x_sems[t], n_dma=2)

            def fb(e, t=t, r0=r0):
                return [e.dma_start(out=xs[t][:, 1536:2048], in_=x_d[r0:r0 + 128, 1536:2048])]
            op("pool", fb, writes=[("xs", t, 3)], dma_sem=xb_sems[t], n_dma=1)

        for t in range(NT):
            load_x(0, t)
        for c0 in (K0, 0, 512, GA0, GA0 + 512):
            cast("in", win_d, winb_d, c0, c0 + 512, D)
        for half in range(2):
            for base in (CC0, CH0, CB0, GC0):
                cast("in", win_d, winb_d, base + half * 512, base + half * 512 + 512, D)
        for c0 in range(0, D, 512):
            cast("out", wout_d, woutb_d, c0, c0 + 512, D)
        cast("proj", wproj_d, wprojb_d, 0, D, PLE)
        for c0 in range(0, D, 512):
            cast("gate", wgate_d, wgateb_d, c0, c0 + 512, D)
        need_upto = [0, 1, 2, 3, 4, 8, 8, 8, 8, 12, 12, 12, 12, 13, 14, 15, 16, 18, 19, 20, 21]
        issue_casts(1)

        def cast_key_for(name, c):
            blk = (c // 512) * 512 if name != "proj" else 0
            if name == "in" and c >= CB0:
                base = CB0 + ((c - CB0) // 1024) * 1024
                blk = base + ((c - base) // 512) * 512
            return cast_keys[(name, blk)]

        op("pool", lambda e: e.memset(neghalf[:], -0.5), writes=["neghalf"])
        op("pool", lambda e: e.memset(ucarry[:], 0.0), writes=["ucarry"])
        for i in range(5):
            op("pool", lambda e, i=i: e.memset(vr[i][:], 1.0), writes=[("vr", i)])
        op("dve", lambda e: e.tensor_scalar(out=omr[:], in0=rotm[:], scalar1=-1.0, scalar2=1.0,
                                            op0=ALU.mult, op1=ALU.add), reads=["const"], writes=["omr"])
        op("dve", lambda e: e.scalar_tensor_tensor(out=qvec[:], in0=gqc[:], scalar=omr[:, 0:1], in1=rotm[:],
                                                   op0=ALU.mult, op1=ALU.add), reads=["const", "omr"],
           writes=["qvec"])
        op("dve", lambda e: e.scalar_tensor_tensor(out=kvec[:], in0=gkc[:], scalar=omr[:, 0:1], in1=rotm[:],
                                                   op0=ALU.mult, op1=ALU.add), reads=["const", "omr"],
           writes=["kvec"])
        for pre, gb in (("q", gqb), ("k", gkb)):
            for nm, tab, g0 in ((pre + "CA", cost, 0), (pre + "SA", sint, 8), (pre + "CB", cost, 8),
                                (pre + "SB", sint, 0)):
                def f(e, nm=nm, tab=tab, g0=g0, gb=gb):
                    return e.tensor_tensor(
                        out=rtab[nm][:].rearrange("p (b f) -> p b f", b=32),
                        in0=tab[:].rearrange("p (b f) -> p b f", b=32),
                        in1=gb[:, g0:g0 + 8].unsqueeze(1).broadcast_to([128, 32, 8]), op=ALU.mult)
                op("dve", f, reads=["const"], writes=["rtab"])
        op("act", lambda e: e.activation(out=es2[:], in_=sinkb[:], func=AF.Exp), reads=["const"], writes=["es2"])
        op("dve", lambda e: e.tensor_scalar(out=es2[:], in0=es2[:], scalar1=2.0, scalar2=None, op0=ALU.mult),
           reads=["es2"], writes=["es2"])
        op("dve", lambda e: e.tensor_scalar(out=cw[:], in0=cw[:], scalar1=0.5, scalar2=None, op0=ALU.mult),
           reads=["const"], writes=["cw"])

        chunk_plan = []
        for c0 in (K0, 0, 512, GA0, GA0 + 512):
            chunk_plan.append(("in", "tok", c0))
        for j in range(8):
            chunk_plan.append(("in", "conv", j))
        for c0 in range(0, D, 512):
            chunk_plan.append(("out", "tok", c0))
        for c0 in range(0, D, 512):
            chunk_plan.append(("gate", "tok", c0))
        NCH = len(chunk_plan)
        wsrc = {"in": winb_d, "out": woutb_d, "gate": wgateb_d}
        state = {"issued": 0}

        def issue_chunk(gidx):
            if gidx >= NS * NCH:
                return
            name, kind, col = chunk_plan[gidx % NCH]
            slot = gidx % 2
            src = wsrc[name]
            if kind == "tok":
                def f(e, src=src, col=col, slot=slot):
                    v = src.rearrange("(c p) n -> p c n", p=128)
                    return [e.dma_start(out=w3[slot][:, 0:8, :], in_=v[:, 0:8, col:col + 512]),
                            e.dma_start(out=w3[slot][:, 8:16, :], in_=v[:, 8:16, col:col + 512])]
                op("sp", f, reads=[cast_key_for(name, col)], writes=[("w", slot)], dma_sem=w_sems[slot], n_dma=2)
            else:
                j = col

                def f(e, src=src, j=j, slot=slot):
                    v = src.rearrange("(c p) n -> p c n", p=128)
                    r = []
                    for si, base in enumerate((CC0, CH0, CB0, GC0)):
                        for hh in range(2):
                            r.append(e.dma_start(out=w3[slot][:, hh * 8:(hh + 1) * 8, si * 128:(si + 1) * 128],
                                                 in_=v[:, hh * 8:(hh + 1) * 8, base + j * 128: base + (j + 1) * 128]))
                    return r
                rk = [cast_key_for("in", base + j * 128) for base in (CC0, CH0, CB0, GC0)]
                op("sp", f, reads=rk, writes=[("w", slot)], dma_sem=w_sems[slot], n_dma=8)

        def next_chunk():
            g = state["issued"]
            if g < NCH:
                issue_casts(need_upto[min(g + 2, NCH - 1)])
            if g == 0:
                issue_chunk(0)
            issue_chunk(g + 1)
            state["issued"] = g + 1
            return g % 2

        bank_rot = {"i": 0}

        def zbank():
            b = bank_rot["i"] % 4
            bank_rot["i"] += 1
            return b

        def bk(i):
            return [("bk", i, 0), ("bk", i, 1)]

        TB = 4
        fs_rot = {"i": 0}

        def rmsnorm_transpose(src, src_key, t, gT, dstT3, dst_key, ssc, msc, rsc, tag):
            xb = xn[t % 2]
            xk = ("xn", t % 2)
            kss, kms, krs = (tag, "ss", t), (tag, "ms", t), (tag, "rs", t)
            op("act", lambda e: e.activation(out=xb[:], in_=src[:], func=AF.Square, accum_out=ssc),
               reads=list(src_key), writes=[xk, kss])
            op("pool", lambda e: e.tensor_scalar(out=msc, in0=ssc, scalar1=1.0 / D, scalar2=EPS, op0=ALU.mult,
                                                 op1=ALU.add), reads=[kss], writes=[kms])
            op("pool", lambda e: e.tensor_tensor(out=rsc, in0=msc, in1=neghalf[:, 0:1], op=ALU.pow),
               reads=[kms, "neghalf"], writes=[krs])
            op("act", lambda e: e.activation(out=xb[:], in_=src[:], func=AF.Identity, scale=rsc),
               reads=list(src_key) + [krs], writes=[xk])
            for half in range(2):
                tb = (TB, 5)[half]

                def tr(e, half=half, tb=tb):
                    r = None
                    for k in range(8):
                        c = half * 8 + k
                        r = e.transpose(out=bankb[tb][:, k * 128:(k + 1) * 128], in_=xb[:, c * 128:(c + 1) * 128],
                                        identity=ident[:])
                    return r
                op("pe", tr, reads=[xk, "const"], writes=bk(tb))

                def ev(e, half=half, tb=tb):
                    return e.tensor_tensor(
                        out=dstT3[:, half * 8:(half + 1) * 8, t * 128:(t + 1) * 128],
                        in0=bankb[tb][:, 0:1024].rearrange("p (c k) -> p c k", c=8),
                        in1=gT[:, half * 8:(half + 1) * 8].unsqueeze(2).broadcast_to([128, 8, 128]), op=ALU.mult)
                op("dve", ev, reads=bk(tb) + ["const"], writes=[(dst_key, half, t)])

        def mm_tok(bank, lT3, t, slot, lkeys, kcs=KC, w_ap3=None, wkey=None, start=True, stop=True):
            w_ap3 = w3[slot] if w_ap3 is None else w_ap3
            wkey = ("w", slot) if wkey is None else wkey

            def f(e):
                r = None
                for c in range(kcs):
                    r = e.matmul(banks[bank][:, 0:512], lhsT=lT3[:, c, t * 128:(t + 1) * 128], rhs=w_ap3[:, c, 0:512],
                                 start=(start and c == 0), stop=(stop and c == kcs - 1))
                return r
            op("pe", f, reads=lkeys + [wkey], writes=bk(bank))

        hkeys = lambda t: [("hT", 0, t), ("hT", 1, t)]
        hkeys_all = [("hT", h, t) for h in range(2) for t in range(NT)]

        def rope(src3, H, pre, b, dst_fn, tmp, dst_keys):
            bc = lambda nm: rtab[nm][:, b * 8:(b + 1) * 8].unsqueeze(1).broadcast_to([128, H, 8])
            t1 = tmp[:, 0:H * 8].rearrange("p (h f) -> p h f", h=H)
            t2 = tmp[:, 128:128 + H * 8].rearrange("p (h f) -> p h f", h=H)
            a = src3[:, :, 0:8]
            bb = src3[:, :, 8:16]
            rk = ["rtab", "ropesrc"]
            op("dve", lambda e: e.tensor_tensor(out=t1, in0=a, in1=bc(pre + "CA"), op=ALU.mult), reads=rk, writes=["t1"])
            op("dve", lambda e: e.tensor_tensor(out=t2, in0=bb, in1=bc(pre + "SA"), op=ALU.mult), reads=rk, writes=["t2"])
            op("dve", lambda e: e.tensor_tensor(out=dst_fn(0), in0=t1, in1=t2, op=ALU.subtract),
               reads=["t1", "t2"], writes=dst_keys)
            op("dve", lambda e: e.tensor_tensor(out=t1, in0=bb, in1=bc(pre + "CB"), op=ALU.mult), reads=rk, writes=["t1"])
            op("dve", lambda e: e.tensor_tensor(out=t2, in0=a, in1=bc(pre + "SB"), op=ALU.mult), reads=rk, writes=["t2"])
            op("dve", lambda e: e.tensor_tensor(out=dst_fn(1), in0=t1, in1=t2, op=ALU.add),
               reads=["t1", "t2"], writes=dst_keys)

        def qk_norm(bank, col0, H, ssc, msc, rsc, fsq):
            ncol = H * 64
            op("act", lambda e: e.activation(out=fsq[:, 0:ncol], in_=banks[bank][:, col0:col0 + ncol], func=AF.Square),
               reads=bk(bank), writes=[("fs", id(fsq))])
            op("dve", lambda e: e.tensor_reduce(out=ssc, in_=fsq[:, 0:ncol].rearrange("p (h d) -> p h d", h=H),
                                                axis=AX.X, op=ALU.add), reads=[("fs", id(fsq))],
               writes=[("sm", id(ssc))])
            op("pool", lambda e: e.tensor_scalar(out=msc, in0=ssc, scalar1=1.0 / 64, scalar2=EPS, op0=ALU.mult,
                                                 op1=ALU.add), reads=[("sm", id(ssc))], writes=[("sm", id(msc))])
            op("pool", lambda e: e.tensor_tensor(out=rsc, in0=msc, in1=neghalf[:, 0:H], op=ALU.pow),
               reads=[("sm", id(msc)), "neghalf"], writes=[("sm", id(rsc))])

        ss1 = scol(4); ms1 = scol(4); rs1 = scol(4)
        ss2 = scol(4); ms2 = scol(4); rs2 = scol(4)
        ssk = scol(4); msk = scol(4); rsk = scol(4)
        ssq = [scol(8), scol(8)]; msq = [scol(8), scol(8)]; rsq = [scol(8), scol(8)]
        den = scol(4); rden = scol(4)
        ss3 = scol(16); ss3t = scol(4); ms3 = scol(4); rs3 = scol(4)

        store_ops = []

        def norm_part(src, src_key, t, ssc, msc, rsc, tag):
            xb = xn[t % 2]
            xk = ("xn", t % 2)
            kss, kms, krs = (tag, "ss", t), (tag, "ms", t), (tag, "rs", t)
            op("act", lambda e: e.activation(out=xb[:], in_=src[:], func=AF.Square, accum_out=ssc),
               reads=list(src_key), writes=[xk, kss])
            op("pool", lambda e: e.tensor_scalar(out=msc, in0=ssc, scalar1=1.0 / D, scalar2=EPS, op0=ALU.mult,
                                                 op1=ALU.add), reads=[kss], writes=[kms])
            op("pool", lambda e: e.tensor_tensor(out=rsc, in0=msc, in1=neghalf[:, 0:1], op=ALU.pow),
               reads=[kms, "neghalf"], writes=[krs])
            op("act", lambda e: e.activation(out=xb[:], in_=src[:], func=AF.Identity, scale=rsc),
               reads=list(src_key) + [krs], writes=[xk])

        def trans_part(t, gT, dstT3, dst_key):
            xb = xn[t % 2]
            xk = ("xn", t % 2)
            for half in range(2):
                tb = (TB, 5)[half]

                def tr(e, half=half, tb=tb):
                    r = None
                    for k in range(8):
                        c = half * 8 + k
                        r = e.transpose(out=bankb[tb][:, k * 128:(k + 1) * 128], in_=xb[:, c * 128:(c + 1) * 128],
                                        identity=ident[:])
                    return r
                op("pe", tr, reads=[xk, "const"], writes=bk(tb))

                def ev(e, half=half, tb=tb):
                    return e.tensor_tensor(
                        out=dstT3[:, half * 8:(half + 1) * 8, t * 128:(t + 1) * 128],
                        in0=bankb[tb][:, 0:1024].rearrange("p (c k) -> p c k", c=8),
                        in1=gT[:, half * 8:(half + 1) * 8].unsqueeze(2).broadcast_to([128, 8, 128]), op=ALU.mult)
                op("dve", ev, reads=bk(tb) + ["const"], writes=[(dst_key, half, t)])

        gidx = {"i": 0}

        def kv_M(s, t, slot):
            b = s * NT + t
            ring = b % 5
            gi = gidx["i"] % 4
            gidx["i"] += 1
            buf = knq[gi]
            bkey = ("knq", gi)
            zb = zbank()
            mm_tok(zb, hT3, t, slot, hkeys(t))
            fsq = fsc[fs_rot["i"] % 2]
            fs_rot["i"] += 1
            qk_norm(zb, 0, 4, ssk, msk, rsk, fsq)
            pk = banks[zb][:, 0:256].rearrange("p (h d) -> p h d", h=4)
            kn4 = buf[:].rearrange("p (h a d) -> p h a d", h=4, a=2)
            op("dve", lambda e: e.tensor_tensor(
                out=kn4, in0=pk.unsqueeze(2).broadcast_to([128, 4, 2, 64]),
                in1=rsk.unsqueeze(2).unsqueeze(3).broadcast_to([128, 4, 2, 64]), op=ALU.mult),
               reads=bk(zb) + [("sm", id(rsk))], writes=[bkey])
            kr3 = rope_t[:, 256:320].rearrange("p (h f) -> p h f", h=4)
            op("dve", lambda e: e.tensor_tensor(
                out=kr3, in0=pk[:, :, 0:16], in1=rsk.unsqueeze(2).broadcast_to([128, 4, 16]), op=ALU.mult),
               reads=bk(zb) + [("sm", id(rsk))], writes=["ropesrc"])
            kro = rope_t[:, 320:384].rearrange("p (h f) -> p h f", h=4)
            rope(kr3, 4, "k", b, lambda lo: kro[:, :, lo * 8:(lo + 1) * 8], rope_t, ["ropedst"])
            op("dve", lambda e: e.tensor_copy(
                out=kn4[:, :, :, 0:16], in_=kro.unsqueeze(2).broadcast_to([128, 4, 2, 16])),
               reads=["ropedst"], writes=[bkey])
            op("act", lambda e: e.activation(
                out=vr[ring][:].rearrange("p (h d) -> p h d", h=4)[:, :, 0:64],
                in_=banks[zb][:, 256:512].rearrange("p (h d) -> p h d", h=4), func=AF.Copy),
               reads=bk(zb), writes=[("vr", ring)])

            def X():
                def ktr(e):
                    r = None
                    for h in range(4):
                        r = e.transpose(out=bankb[TB][:, h * 128:(h + 1) * 128], in_=buf[:, h * 128:(h + 1) * 128],
                                        identity=ident[:])
                    return r
                op("pe", ktr, reads=[bkey, "const"], writes=bk(TB))
                op("act", lambda e: e.activation(out=kTr[ring][:], in_=bankb[TB][:, 0:512],
                                                 func=AF.Identity, scale=kvec[:, 0:1]),
                   reads=bk(TB) + ["kvec"], writes=[("kTr", ring)])
            return X

        def q_M(s, t, qc, slot):
            b = s * NT + t
            gi = gidx["i"] % 4
            gidx["i"] += 1
            buf = knq[gi]
            bkey = ("knq", gi)
            zb = zbank()
            mm_tok(zb, hT3, t, slot, hkeys(t))
            fsq = fsc[fs_rot["i"] % 2]
            fs_rot["i"] += 1
            rq = rsq[qc]
            qk_norm(zb, 0, 8, ssq[qc], msq[qc], rq, fsq)
            pq = banks[zb][:, 0:512].rearrange("p (h d) -> p h d", h=8)
            qn3 = buf[:].rearrange("p (h d) -> p h d", h=8)
            op("dve", lambda e: e.tensor_tensor(
                out=qn3, in0=pq, in1=rq.unsqueeze(2).broadcast_to([128, 8, 64]), op=ALU.mult),
               reads=bk(zb) + [("sm", id(rq))], writes=[bkey])
            qr3 = rope_t[:, 384:512].rearrange("p (h f) -> p h f", h=8)
            op("dve", lambda e: e.tensor_tensor(
                out=qr3, in0=pq[:, :, 0:16], in1=rq.unsqueeze(2).broadcast_to([128, 8, 16]), op=ALU.mult),
               reads=bk(zb) + [("sm", id(rq))], writes=["ropesrc"])
            rope(qr3, 8, "q", b, lambda lo: qn3[:, :, lo * 8:(lo + 1) * 8], rope_t, [bkey])

            def X():
                def qtr(e):
                    r = None
                    for k in range(4):
                        r = e.transpose(out=bankb[TB][:, k * 128:(k + 1) * 128],
                                        in_=buf[:, k * 128:(k + 1) * 128], identity=ident[:])
                    return r
                op("pe", qtr, reads=[bkey, "const"], writes=bk(TB))
                op("act", lambda e: e.activation(
                    out=qT[t][:, qc * 512:(qc + 1) * 512], in_=bankb[TB][:, 0:512], func=AF.Identity,
                    scale=qvec[:, 0:1]), reads=bk(TB) + ["qvec"], writes=[("qT", t, qc)])
            return X

        def g_M(t, gc, slot):
            zb = zbank()
            mm_tok(zb, hT3, t, slot, hkeys(t))
            th = fsc[2 + fs_rot["i"] % 2]
            fs_rot["i"] += 1
            op("act", lambda e: e.activation(out=th[:, 0:512], in_=banks[zb][:, 0:512],
                                             func=AF.Tanh, scale=0.5),
               reads=bk(zb), writes=[("fs", id(th))])
            op("dve", lambda e: e.scalar_tensor_tensor(
                out=sgl[t][:, gc * 512:(gc + 1) * 512], in0=th[:, 0:512], scalar=1.0, in1=banks[zb][:, 0:512],
                op0=ALU.add, op1=ALU.mult), reads=bk(zb) + [("fs", id(th))], writes=[("sgl", t, gc)])

        def att_sc(s, t, h):
            b = s * NT + t
            rc, rp = b % 5, (b - 1) % 5
            first = (b == 0)

            def sc(e):
                r = None
                for kb, ring in ((0, rp), (1, rc)):
                    if first and kb == 0:
                        continue
                    for half, bank in ((0, 5), (1, 6)):
                        r0 = half * 64
                        r = e.matmul(banks[bank][:, kb * 256:(kb + 1) * 256],
                                     lhsT=kTr[ring][r0:r0 + 64, h * 128:(h + 1) * 128],
                                     rhs=qT[t][r0:r0 + 64, h * 256:(h + 1) * 256], start=True, stop=True)
                return r
            rd = [("kTr", rc), ("qT", t, h // 2)] + ([] if first else [("kTr", rp)])
            op("pe", sc, reads=rd, writes=bk(5) + bk(6))
            c0 = 256 if first else 0
            for half, bank in ((0, 5), (1, 6)):
                pb_i = (2 * h + half) % 4
                pbuf = Pb[pb_i]
                op("act", lambda e, pbuf=pbuf, bank=bank: e.activation(
                    out=pbuf[:, c0:512], in_=banks[bank][:, c0:512], func=AF.Exp, scale=0.125),
                   reads=bk(bank), writes=[("Pb", pb_i)])
                op("dve", lambda e, pbuf=pbuf: e.tensor_tensor(
                    out=pbuf[:, c0:512], in0=pbuf[:, c0:512], in1=maskt[:, c0:512], op=ALU.mult),
                   reads=[("Pb", pb_i), "const"], writes=[("Pb", pb_i)])

        def att_pv(s, t, h):
            b = s * NT + t
            rc, rp = b % 5, (b - 1) % 5
            first = (b == 0)
            mt = mixtok[t % 2]
            pbs = [Pb[(2 * h) % 4], Pb[(2 * h + 1) % 4]]

            def pv(e):
                r = None
                for j in range(2):
                    for half in range(2):
                        hl = 2 * j + half
                        for kb, ring in ((0, rp), (1, rc)):
                            if first and kb == 0:
                                continue
                            r = e.matmul(banks[7][:, hl * 65:(hl + 1) * 65],
                                         lhsT=pbs[half][:, kb * 256 + j * 128: kb * 256 + (j + 1) * 128],
                                         rhs=vr[ring][:, h * 65:(h + 1) * 65],
                                         start=(kb == 0 or first), stop=(kb == 1))
                return r
            rd = [("Pb", (2 * h) % 4), ("Pb", (2 * h + 1) % 4), ("vr", rc)] + ([] if first else [("vr", rp)])
            op("pe", pv, reads=rd, writes=bk(7))
            po3 = banks[7][:, 0:260].rearrange("p (h d) -> p h d", h=4)
            op("dve", lambda e: e.scalar_tensor_tensor(
                out=den, in0=po3[:, :, 64], scalar=2.0, in1=es2[:, 4 * h:4 * h + 4], op0=ALU.mult,
                op1=ALU.add), reads=bk(7) + ["es2"], writes=["den"])
            op("dve", lambda e: e.reciprocal(out=rden, in_=den), reads=["den"], writes=["rden"])
            tmp = fsc[h % 2]
            op("dve", lambda e: e.tensor_tensor(
                out=tmp[:, 0:256].rearrange("p (h d) -> p h d", h=4), in0=po3[:, :, 0:64],
                in1=rden.unsqueeze(2).broadcast_to([128, 4, 64]), op=ALU.mult),
               reads=bk(7) + ["rden"], writes=[("fs", id(tmp))])
            op("pool", lambda e: e.tensor_tensor(
                out=mt[:, h * 256:(h + 1) * 256], in0=tmp[:, 0:256], in1=sgl[t][:, h * 256:(h + 1) * 256],
                op=ALU.mult), reads=[("fs", id(tmp)), ("sgl", t, h // 2)], writes=[("mixtok", t % 2)])

        def att_fin(t):
            mt = mixtok[t % 2]

            def mtr(e):
                r = None
                for k in range(8):
                    r = e.transpose(out=bankb[TB][:, k * 128:(k + 1) * 128], in_=mt[:, k * 128:(k + 1) * 128],
                                    identity=ident[:])
                return r
            op("pe", mtr, reads=[("mixtok", t % 2), "const"], writes=bk(TB))
            op("act", lambda e: e.activation(
                out=mixT3[:, 0:8, t * 128:(t + 1) * 128],
                in_=bankb[TB][:, 0:1024].rearrange("p (c k) -> p c k", c=8), func=AF.Copy),
               reads=bk(TB), writes=[("mixT", "a", t)])

        def conv_seg(slot, si, bank):
            def mmseg(e):
                r = None
                for c in range(KC):
                    r = e.matmul(banks[bank][:, 0:512], lhsT=w3[slot][:, c, si * 128:(si + 1) * 128],
                                 rhs=hT3[:, c, 0:512], start=(c == 0), stop=(c == KC - 1))
                return r
            op("pe", mmseg, reads=hkeys_all + [("w", slot)], writes=bk(bank))

        def conv_u(j):
            ch = fsc[j % 2]
            u = fsc[4 + j % 2]
            op("act", lambda e: e.activation(out=ch[:, 0:512], in_=banks[1][:, 0:512], func=AF.Copy),
               reads=bk(1), writes=[("fs", id(ch))])
            op("dve", lambda e: e.tensor_copy(out=u[:, 0:2], in_=ucarry[:, 2 * j:2 * j + 2]),
               reads=["ucarry%d" % j], writes=[("fs", id(u))])
            op("dve", lambda e: e.tensor_tensor(out=u[:, 2:514], in0=banks[0][:, 0:512], in1=ch[:, 0:512],
                                                op=ALU.mult), reads=bk(0) + [("fs", id(ch))],
               writes=[("fs", id(u))])
            op("dve", lambda e: e.tensor_copy(out=ucarry[:, 2 * j:2 * j + 2], in_=u[:, 512:514]),
               reads=[("fs", id(u))], writes=["ucarry%d" % j])

        def conv_fin(j):
            u = fsc[4 + j % 2]
            acc = fsc[6 + j % 2]
            th = fsc[2 + j % 2]
            bcb, bg = 2, 3
            op("dve", lambda e: e.tensor_scalar(out=acc[:, 0:512], in0=u[:, 2:514], scalar1=cw[:, 3 * j + 2:3 * j + 3],
                                                 scalar2=None, op0=ALU.mult), reads=[("fs", id(u)), "cw"],
               writes=[("fs", id(acc))])
            op("dve", lambda e: e.scalar_tensor_tensor(out=acc[:, 0:512], in0=u[:, 1:513],
                                                        scalar=cw[:, 3 * j + 1:3 * j + 2], in1=acc[:, 0:512],
                                                        op0=ALU.mult, op1=ALU.add),
               reads=[("fs", id(u)), "cw", ("fs", id(acc))], writes=[("fs", id(acc))])
            op("dve", lambda e: e.scalar_tensor_tensor(out=acc[:, 0:512], in0=u[:, 0:512],
                                                        scalar=cw[:, 3 * j:3 * j + 1], in1=acc[:, 0:512],
                                                        op0=ALU.mult, op1=ALU.add),
               reads=[("fs", id(u)), "cw", ("fs", id(acc))], writes=[("fs", id(acc))])
            op("dve", lambda e: e.tensor_tensor(out=acc[:, 0:512], in0=acc[:, 0:512], in1=banks[bcb][:, 0:512],
                                                op=ALU.mult), reads=bk(bcb) + [("fs", id(acc))],
               writes=[("fs", id(acc))])
            op("act", lambda e: e.activation(out=th[:, 0:512], in_=banks[bg][:, 0:512], func=AF.Tanh, scale=0.5),
               reads=bk(bg), writes=[("fs", id(th))])
            op("dve", lambda e: e.scalar_tensor_tensor(out=th[:, 0:512], in0=th[:, 0:512], scalar=1.0,
                                                       in1=banks[bg][:, 0:512], op0=ALU.add, op1=ALU.mult),
               reads=bk(bg) + [("fs", id(th))], writes=[("fs", id(th))])
            op("dve", lambda e: e.tensor_tensor(out=mixT3[:, 8 + j, :], in0=acc[:, 0:512], in1=th[:, 0:512],
                                                op=ALU.mult), reads=[("fs", id(acc)), ("fs", id(th))],
               writes=[("mixT", "c", j)])

        def p_load(s, t):
            r0 = s * T + t * 128
            pi = t % 2
            op("pool", lambda e: [e.dma_start(out=psb[pi][:], in_=p_d[r0:r0 + 128, :])],
               writes=[("psb", pi)], dma_sem=p_sems[pi])
            op("act", lambda e: e.activation(out=pbf[t][:], in_=psb[pi][:], func=AF.Copy),
               reads=[("psb", pi)], writes=[("pbf", t)])

        def p_trans(t):
            def ptr(e):
                r = None
                for k in range(2):
                    r = e.transpose(out=bankb[6][:, k * 128:(k + 1) * 128], in_=pbf[t][:, k * 128:(k + 1) * 128],
                                    identity=ident[:])
                return r
            op("pe", ptr, reads=[("pbf", t), "const"], writes=bk(6))
            op("act", lambda e: e.activation(
                out=pT3[:, :, t * 128:(t + 1) * 128],
                in_=bankb[6][:, 0:256].rearrange("p (c k) -> p c k", c=2), func=AF.Copy),
               reads=bk(6), writes=[("pT", t)])

        def e1_stats(t):
            for n in range(4):
                zb = zbank()
                mm_tok(zb, pT3, t, None, [("pT", t)], kcs=2, w_ap3=wproj3[:, :, n * 512:(n + 1) * 512],
                       wkey="wproj")
                jk = fsc[fs_rot["i"] % 2]
                fs_rot["i"] += 1
                op("act", lambda e, jk=jk, zb=zb, n=n: e.activation(
                    out=jk[:, 0:512], in_=banks[zb][:, 0:512], func=AF.Square,
                    accum_out=ss3[:, t * 4 + n:t * 4 + n + 1]), reads=bk(zb),
                   writes=[("fs", id(jk)), ("ss3", t)])
            op("dve", lambda e: e.tensor_reduce(out=ss3t[:, t:t + 1], in_=ss3[:, t * 4:t * 4 + 4],
                                                axis=AX.X, op=ALU.add), reads=[("ss3", t)],
               writes=[("ss3t", t)])
            op("pool", lambda e: e.tensor_scalar(out=ms3[:, t:t + 1], in0=ss3t[:, t:t + 1],
                                                 scalar1=1.0 / D, scalar2=EPS, op0=ALU.mult, op1=ALU.add),
               reads=[("ss3t", t)], writes=[("ms3", t)])
            op("pool", lambda e: e.tensor_tensor(out=rs3[:, t:t + 1], in0=ms3[:, t:t + 1],
                                                 in1=neghalf[:, 0:1], op=ALU.pow),
               reads=[("ms3", t), "neghalf"], writes=[("rs3", t)])
            op("pool", lambda e: e.tensor_scalar(out=rs3[:, t:t + 1], in0=rs3[:, t:t + 1], scalar1=0.5,
                                                 scalar2=None, op0=ALU.mult),
               reads=[("rs3", t)], writes=[("rs3", t)])

        for s in range(NS):
            nA = lambda t: norm_part(xs[t], [("xs", t, n_) for n_ in range(4)], t, ss1[:, t:t + 1], ms1[:, t:t + 1], rs1[:, t:t + 1], "n1")
            if s == 0:
                nA(0)
                nA(1)
            pend = []
            LAG = 3
            slot = next_chunk()
            trans_part(0, g1T, hT3, "hT")
            nA(2)
            trans_part(1, g1T, hT3, "hT")
            nA(3)
            for t in range(NT):
                pend.append(kv_M(s, t, slot))
                if len(pend) > LAG:
                    pend.pop(0)()
                if t + 2 < NT:
                    trans_part(t + 2, g1T, hT3, "hT")
            for qc in range(2):
                slot = next_chunk()
                for t in range(NT):
                    pend.append(q_M(s, t, qc, slot))
                    if len(pend) > LAG:
                        pend.pop(0)()
            for gc in range(2):
                slot = next_chunk()
                for t in range(NT):
                    g_M(t, gc, slot)
                    if pend:
                        pend.pop(0)()
            while pend:
                pend.pop(0)()

            for t in range(NT):
                p_load(s, t)
            for j in range(8):
                slot = next_chunk()
                t = j // 2
                h0 = 2 * (j % 2)
                conv_seg(slot, 0, 0)
                if j % 2 == 0 and j > 0:
                    att_fin(t - 1)
                att_sc(s, t, h0)
                conv_seg(slot, 1, 1)
                conv_u(j)
                att_pv(s, t, h0)
                att_sc(s, t, h0 + 1)
                conv_seg(slot, 2, 2)
                conv_seg(slot, 3, 3)
                att_pv(s, t, h0 + 1)
                conv_fin(j)
            att_fin(NT - 1)

            mkeys_all = [("mixT", "c", j) for j in range(8)]
            for n in range(4):
                slot = next_chunk()
                for t in range(NT):
                    zb = zbank()
                    mm_tok(zb, mixT3, t, slot, [("mixT", "a", t)] + mkeys_all)
                    op("dve", lambda e, t=t, n=n, zb=zb: e.tensor_tensor(
                        out=xs[t][:, n * 512:(n + 1) * 512], in0=xs[t][:, n * 512:(n + 1) * 512],
                        in1=banks[zb][:, 0:512], op=ALU.add), reads=bk(zb) + [("xs", t, n)], writes=[("xs", t, n)])
                    if n == 0:
                        p_trans(t)
                    if n == 3 and t < 2:
                        pass

            if s == 0:
                issue_casts(17)
                op("sp", lambda e: [e.dma_start(out=wproj3, in_=wprojb_d.rearrange("(c p) n -> p c n", p=128))],
                   reads=[cast_keys[("proj", 0)]], writes=["wproj"], dma_sem=s_wproj)
            nD = lambda t: norm_part(xs[t], [("xs", t, n_) for n_ in range(4)], t, ss2[:, t:t + 1], ms2[:, t:t + 1], rs2[:, t:t + 1], "n2")
            nD(0)
            nD(1)
            e1_stats(0)
            e1_stats(1)
            trans_part(0, g2T, hT3, "hT")
            nD(2)
            e1_stats(2)
            trans_part(1, g2T, hT3, "hT")
            nD(3)
            e1_stats(3)
            trans_part(2, g2T, hT3, "hT")
            trans_part(3, g2T, hT3, "hT")

            for n in range(4):
                slot = next_chunk()
                bi = n % 2
                op("sp", lambda e, bi=bi, n=n: [
                    e.dma_start(out=bch[bi][:], in_=bias_d[:, n * 512:(n + 1) * 512]),
                    e.dma_start(out=pch[bi][:], in_=pg_d[:, n * 512:(n + 1) * 512])],
                   writes=[("bp", bi)], dma_sem=bp_sems[bi], n_dma=2)
                for t in range(NT):
                    zg = zbank()
                    mm_tok(zg, hT3, t, slot, hkeys(t))
                    ze = zbank()
                    mm_tok(ze, pT3, t, None, [("pT", t)], kcs=2, w_ap3=wproj3[:, :, n * 512:(n + 1) * 512],
                           wkey="wproj")
                    gp = fsc[fs_rot["i"] % 2]
                    ge = fsc[4 + fs_rot["i"] % 2]
                    fs_rot["i"] += 1
                    op("dve", lambda e, gp=gp, zg=zg, bi=bi: e.tensor_tensor(
                        out=gp[:, 0:512], in0=banks[zg][:, 0:512], in1=bch[bi][:], op=ALU.add),
                       reads=bk(zg) + [("bp", bi)], writes=[("fs", id(gp))])
                    op("act", lambda e, gp=gp: e.activation(out=gp[:, 0:512], in_=gp[:, 0:512], func=AF.Tanh, scale=0.5),
                       reads=[("fs", id(gp))], writes=[("fs", id(gp))])
                    op("dve", lambda e, gp=gp, ge=ge, ze=ze: e.scalar_tensor_tensor(
                        out=ge[:, 0:512], in0=gp[:, 0:512], scalar=1.0, in1=banks[ze][:, 0:512], op0=ALU.add,
                        op1=ALU.mult), reads=bk(ze) + [("fs", id(gp))], writes=[("fs", id(ge))])
                    op("dve", lambda e, ge=ge, bi=bi: e.tensor_tensor(
                        out=ge[:, 0:512], in0=ge[:, 0:512], in1=pch[bi][:], op=ALU.mult),
                       reads=[("fs", id(ge)), ("bp", bi)], writes=[("fs", id(ge))])
                    op("dve", lambda e, ge=ge, t=t, n=n: e.scalar_tensor_tensor(
                        out=xs[t][:, n * 512:(n + 1) * 512], in0=ge[:, 0:512], scalar=rs3[:, t:t + 1],
                        in1=xs[t][:, n * 512:(n + 1) * 512], op0=ALU.mult, op1=ALU.add),
                       reads=[("fs", id(ge)), ("rs3", t), ("xs", t, n)], writes=[("xs", t, n)])
                    r0 = s * T + t * 128

                    def st(e, t=t, r0=r0, n=n):
                        return [e.dma_start(out=y_d[r0:r0 + 128, n * 512:(n + 1) * 512],
                                            in_=xs[t][:, n * 512:(n + 1) * 512])]
                    store_ops.append(op("sp", st, reads=[("xs", t, n)], dma_sem=y_sems[t], n_dma=1))
                    if n == 3:
                        if s + 1 < NS:
                            load_x(s + 1, t)
                            nAn = lambda tt: norm_part(xs[tt], [("xs", tt, n_) for n_ in range(4)], tt, ss1[:, tt:tt + 1], ms1[:, tt:tt + 1],
                                                       rs1[:, tt:tt + 1], "n1")
                            if t == 2:
                                nAn(0)
                            if t == 3:
                                nAn(1)
        fin = op("pool", None)
        fin.deps = list(store_ops[-4 * NT:])
        for o in fin.deps:
            o.has_dep = True

        with nc.Block() as block:
            Sd.emit(block, qs, dict(pe="tensor", act="scalar", dve="vector", pool="gpsimd", sp="sync"))
    return nc


_CACHE = {}


def _consts():
    half = 8
    inv_freq = np.power(np.float32(500000.0), -np.arange(half, dtype=np.float32) * np.float32(2.0) / np.float32(16))
    pos = np.arange(S, dtype=np.float32)
    ang = pos[:, None] * inv_freq[None, :]
    cos = np.cos(ang).astype(np.float32)
    sin = np.sin(ang).astype(np.float32)
    cost = np.ascontiguousarray(cos.reshape(32, 128, 8).transpose(1, 0, 2)).reshape(128, 256)
    sint = np.ascontiguousarray(sin.reshape(32, 128, 8).transpose(1, 0, 2)).reshape(128, 256)
    ident = np.eye(128, dtype=np.float32).astype(ml_dtypes.bfloat16)
    j = np.arange(128)[:, None]
    i = np.arange(128)[None, :]
    prev = (i < j).astype(np.float32)
    cur = (i >= j).astype(np.float32)
    mask = np.concatenate([prev, prev, cur, cur], axis=1).astype(ml_dtypes.bfloat16)
    rotm = ((np.arange(128) % 64) < 16).astype(np.float32).reshape(128, 1)
    return cost, sint, ident, mask, rotm


def kernel(x, p, norm_gain, w_in, q_norm_gain, k_norm_gain, attn_sinks, conv_w, w_out,
           ple_gate_norm_gain, w_ple_gate, b_ple_gate, w_ple_proj, ple_norm_gain):
    f = lambda a: np.ascontiguousarray(np.asarray(a, dtype=np.float32))
    x = f(x); p = f(p)
    cost, sint, ident, mask, rotm = _consts()
    ng = f(norm_gain)[0]
    g2 = f(ple_gate_norm_gain)[0]
    gq = f(q_norm_gain)[0]
    gk = f(k_norm_gain)[0]
    cwv = f(conv_w)[0]
    shared = dict(
        w_in=f(w_in)[0], w_out=f(w_out)[0], w_gate=f(w_ple_gate)[0], w_proj=f(w_ple_proj)[0],
        g1T=np.ascontiguousarray(ng.reshape(KC, 128).T), g2T=np.ascontiguousarray(g2.reshape(KC, 128).T),
        gqb=np.ascontiguousarray(np.broadcast_to(gq[None, :], (128, 64))),
        gkb=np.ascontiguousarray(np.broadcast_to(gk[None, :], (128, 64))),
        gqc=np.ascontiguousarray(np.tile(gq, 2).reshape(128, 1)),
        gkc=np.ascontiguousarray(np.tile(gk, 2).reshape(128, 1)),
        rotm=rotm,
        sinkb=np.ascontiguousarray(np.broadcast_to(f(attn_sinks)[0][None, :], (128, 16))),
        cw=np.ascontiguousarray(cwv.reshape(3, 8, 128).transpose(2, 1, 0)).reshape(128, 24),
        biasb=np.ascontiguousarray(np.broadcast_to(f(b_ple_gate)[0][None, :], (128, D))),
        pgb=np.ascontiguousarray(np.broadcast_to(f(ple_norm_gain)[0][None, :], (128, D))),
        cost=cost, sint=sint, ident=ident, maskt=mask,
    )
    if "nc" not in _CACHE:
        _CACHE["nc"] = build_program()
    nc = _CACHE["nc"]
    in_maps = []
    for c in range(N_CORES):
        m = dict(shared)
        m["x"] = x[c]
        m["p"] = p[0, c]
        in_maps.append(m)
    res = run_bass_kernel_spmd(nc, in_maps, core_ids=list(range(N_CORES)))
    out = np.stack([np.asarray(r["y"], dtype=np.float32) for r in res.results], axis=0)
    return out
```

```python
import numpy as np
import ml_dtypes
from contextlib import ExitStack
import concourse.bass as bass
import concourse.mybir as mybir
from concourse.bass_utils import run_bass_kernel_spmd

F32 = mybir.dt.float32
BF16 = mybir.dt.bfloat16
ALU = mybir.AluOpType
AF = mybir.ActivationFunctionType
AX = mybir.AxisListType

N_CORES = 8
D = 2048
S = 4096
T = 512
NT = 4
NS = S // T
KC = 16
PLE = 256
IN_W = 6656
EPS = 1e-6
Q0, K0, V0, GA0, CB0, CC0, CH0, GC0 = 0, 1024, 1280, 1536, 2560, 3584, 4608, 5632


class _Op:
    __slots__ = ("q", "fn", "deps", "has_dep", "ms", "is_dma", "sem", "val")


class Sched:
    QUEUES = ("pe", "act", "dve", "pool", "sp")

    def __init__(self):
        self.queues = {q: [] for q in self.QUEUES}
        self.last_w = {}
        self.readers = {}
        self.dma_vals = {}

    def op(self, q, fn, reads=(), writes=(), dma_sem=None, n_dma=1):
        o = _Op()
        o.q = q
        o.fn = fn
        o.has_dep = False
        o.ms = None
        o.is_dma = dma_sem is not None
        o.sem = dma_sem
        if o.is_dma:
            v = self.dma_vals.get(dma_sem, 0) + 16 * n_dma
            self.dma_vals[dma_sem] = v
            o.val = v
        else:
            o.val = None
        def _canon(r):
            return ("bk", r[1]) if (isinstance(r, tuple) and r[0] == "bk") else r
        reads = [_canon(r) for r in reads]
        writes = [_canon(r) for r in writes]
        writes = writes + [r for r in reads if isinstance(r, tuple) and r[0] == "bk"]
        reads = [r for r in reads if not (isinstance(r, tuple) and r[0] == "bk")]
        deps = []
        raw = set()
        for r in reads:
            w = self.last_w.get(r)
            if w is not None:
                deps.append(w)
                raw.add(id(w))
        for r in writes:
            w = self.last_w.get(r)
            if w is not None:
                deps.append(w)
            deps.extend(self.readers.get(r, ()))
        od = []
        seen = set()
        for d in deps:
            if id(d) in seen:
                continue
            seen.add(id(d))
            if d.q == q and not d.is_dma and (id(d) not in raw or q == "pe"):
                continue
            d.has_dep = True
            od.append(d)
        o.deps = od
        for r in reads:
            self.readers.setdefault(r, []).append(o)
        for r in writes:
            self.last_w[r] = o
            self.readers[r] = []
        self.queues[q].append(o)
        return o

    def emit(self, block, qsems, engines):
        for q in self.QUEUES:
            c = 0
            for o in self.queues[q]:
                if o.has_dep and not o.is_dma:
                    c += 1
                    o.ms = c

        def run(q, eng):
            seen = {}
            for o in self.queues[q]:
                need = {}
                for d in o.deps:
                    if d.is_dma:
                        s, v = d.sem, d.val
                    else:
                        s, v = qsems[d.q], d.ms
                    if seen.get(s, 0) >= v:
                        continue
                    if need.get(s, 0) < v:
                        need[s] = v
                for s, v in need.items():
                    eng.wait_ge(s, v)
                    seen[s] = v
                if o.fn is None:
                    continue
                r = o.fn(eng)
                if o.is_dma:
                    for ins in r:
                        ins.then_inc(o.sem, 16)
                elif o.ms is not None:
                    r.then_inc(qsems[q], 1)

        for q in self.QUEUES:
            if not self.queues[q]:
                continue
            dec = getattr(block, engines[q])

            def mk(q=q):
                def _f(eng):
                    run(q, eng)
                return _f
            dec(mk())


def build_program():
    nc = bass.Bass("TRN2", target_bir_lowering=False)
    dram = lambda n, s, d, k: nc.dram_tensor(n, s, d, kind=k).ap()
    x_d = dram("x", [S, D], F32, "ExternalInput")
    p_d = dram("p", [S, PLE], F32, "ExternalInput")
    win_d = dram("w_in", [D, IN_W], F32, "ExternalInput")
    wout_d = dram("w_out", [D, D], F32, "ExternalInput")
    wgate_d = dram("w_gate", [D, D], F32, "ExternalInput")
    wproj_d = dram("w_proj", [PLE, D], F32, "ExternalInput")
    g1T_d = dram("g1T", [128, KC], F32, "ExternalInput")
    g2T_d = dram("g2T", [128, KC], F32, "ExternalInput")
    gqb_d = dram("gqb", [128, 64], F32, "ExternalInput")
    gkb_d = dram("gkb", [128, 64], F32, "ExternalInput")
    gqc_d = dram("gqc", [128, 1], F32, "ExternalInput")
    gkc_d = dram("gkc", [128, 1], F32, "ExternalInput")
    rotm_d = dram("rotm", [128, 1], F32, "ExternalInput")
    sink_d = dram("sinkb", [128, 16], F32, "ExternalInput")
    cw_d = dram("cw", [128, 24], F32, "ExternalInput")
    bias_d = dram("biasb", [128, D], F32, "ExternalInput")
    pg_d = dram("pgb", [128, D], F32, "ExternalInput")
    cos_d = dram("cost", [128, 256], F32, "ExternalInput")
    sin_d = dram("sint", [128, 256], F32, "ExternalInput")
    ident_d = dram("ident", [128, 128], BF16, "ExternalInput")
    mask_d = dram("maskt", [128, 512], BF16, "ExternalInput")
    y_d = dram("y", [S, D], F32, "ExternalOutput")
    winb_d = dram("w_in_bf", [D, IN_W], BF16, "Internal")
    woutb_d = dram("w_out_bf", [D, D], BF16, "Internal")
    wgateb_d = dram("w_gate_bf", [D, D], BF16, "Internal")
    wprojb_d = dram("w_proj_bf", [PLE, D], BF16, "Internal")

    Sd = Sched()
    op = Sd.op
    es = ExitStack()
    with es:
        sb = lambda n, s, d: es.enter_context(nc.sbuf_tensor(n, s, d))
        sem = lambda n: es.enter_context(nc.semaphore(n))
        qs = {q: sem("q_" + q) for q in Sched.QUEUES}
        banks = [es.enter_context(nc.psum_tensor("bank%d" % i, [128, 512], F32)) for i in range(8)]
        bankb = [b.bitcast(BF16) for b in banks]

        xs = [sb("xs%d" % t, [128, D], F32) for t in range(NT)]
        xn = [sb("xn%d" % i, [128, D], BF16) for i in range(2)]
        hT = sb("hT", [128, KC * T], BF16)
        mixT = sb("mixT", [128, KC * T], BF16)
        wsl = [sb("wsl%d" % i, [128, KC * 512], BF16) for i in range(2)]
        qT = [sb("qT%d" % t, [128, 1024], BF16) for t in range(NT)]
        sgl = [sb("sgl%d" % t, [128, 1024], F32) for t in range(NT)]
        kTr = [sb("kTr%d" % i, [128, 512], BF16) for i in range(5)]
        vr = [sb("vr%d" % i, [128, 4 * 65], BF16) for i in range(5)]
        knq = [sb("knq%d" % i, [128, 512], BF16) for i in range(4)]
        Pb = [sb("Pb%d" % i, [128, 512], BF16) for i in range(4)]
        mixtok = [sb("mixtok%d" % i, [128, 1024], BF16) for i in range(2)]
        fsc = [sb("fsc%d" % i, [128, 516], F32) for i in range(8)]
        bch = [sb("bch%d" % i, [128, 512], F32) for i in range(2)]
        pch = [sb("pch%d" % i, [128, 512], F32) for i in range(2)]
        wproj = sb("wproj", [128, 2 * D], BF16)
        pT = sb("pT", [128, 2 * T], BF16)
        psb = [sb("psb%d" % i, [128, PLE], F32) for i in range(2)]
        pbf = [sb("pbf%d" % i, [128, PLE], BF16) for i in range(4)]
        ident = sb("ident_s", [128, 128], BF16)
        maskt = sb("mask_s", [128, 512], BF16)
        g1T = sb("g1T_s", [128, KC], F32)
        g2T = sb("g2T_s", [128, KC], F32)
        gqb = sb("gqb_s", [128, 64], F32)
        gkb = sb("gkb_s", [128, 64], F32)
        qvec = sb("qvec", [128, 1], F32)
        kvec = sb("kvec", [128, 1], F32)
        gqc = sb("gqc_s", [128, 1], F32)
        gkc = sb("gkc_s", [128, 1], F32)
        rotm = sb("rotm_s", [128, 1], F32)
        omr = sb("omr", [128, 1], F32)
        sinkb = sb("sink_s", [128, 16], F32)
        es2 = sb("es2", [128, 16], F32)
        cw = sb("cw_s", [128, 24], F32)
        cost = sb("cos_s", [128, 256], F32)
        sint = sb("sin_s", [128, 256], F32)
        rtab = {}
        for nm in ("qCA", "qSA", "qCB", "qSB", "kCA", "kSA", "kCB", "kSB"):
            rtab[nm] = sb("rt_" + nm, [128, 256], F32)
        neghalf = sb("neghalf", [128, 16], F32)
        small = sb("small", [128, 256], F32)
        ucarry = sb("ucarry", [128, 16], F32)
        rope_t = sb("rope_t", [128, 512], F32)

        _sc = [0]

        def scol(n):
            c = _sc[0]
            _sc[0] += n
            assert _sc[0] <= 256
            return small[:, c:c + n]

        hT3 = hT[:].rearrange("p (c k) -> p c k", c=KC)
        mixT3 = mixT[:].rearrange("p (c k) -> p c k", c=KC)
        w3 = [w[:].rearrange("p (c k) -> p c k", c=KC) for w in wsl]
        wproj3 = wproj[:].rearrange("p (c k) -> p c k", c=2)
        pT3 = pT[:].rearrange("p (c k) -> p c k", c=2)

        s_const = sem("s_const")
        const_list = [(ident, ident_d), (maskt, mask_d), (g1T, g1T_d), (g2T, g2T_d), (gqb, gqb_d),
                      (gkb, gkb_d), (gqc, gqc_d), (gkc, gkc_d), (rotm, rotm_d), (sinkb, sink_d),
                      (cw, cw_d), (cost, cos_d), (sint, sin_d)]

        def ld_const(e):
            return [e.dma_start(out=a[:], in_=b) for a, b in const_list]
        op("sp", ld_const, writes=["const"], dma_sem=s_const, n_dma=len(const_list))

        cast_keys = {}
        cast_list = []
        cast_state = {"n": 0}

        def cast(name, src, dst, c0, c1, r1):
            cast_list.append((name, src, dst, c0, c1, r1))
            cast_keys[(name, c0)] = ("wbf", name, c0)

        def issue_casts(upto):
            while cast_state["n"] <= min(upto, len(cast_list) - 1):
                i = cast_state["n"]
                name, src, dst, c0, c1, r1 = cast_list[i]
                s_ = sem("s_cast_%s_%d" % (name, c0))
                npc = 4 if r1 >= 512 else 1
                rs = r1 // npc

                def f(e, src=src, dst=dst, c0=c0, c1=c1, npc=npc, rs=rs):
                    return [e.dma_start(out=dst[k * rs:(k + 1) * rs, c0:c1], in_=src[k * rs:(k + 1) * rs, c0:c1],
                                        max_dma_last_dim=8192) for k in range(npc)]
                rd = []
                if i >= 2:
                    pn, _, _, pc0, _, _ = cast_list[i - 2]
                    rd = [("wbf", pn, pc0)]
                op("pool", f, reads=rd, writes=[("wbf", name, c0)], dma_sem=s_, n_dma=npc)
                cast_state["n"] = i + 1

        x_sems = [sem("s_x%d" % t) for t in range(NT)]
        y_sems = [sem("s_y%d" % t) for t in range(NT)]
        p_sems = [sem("s_p%d" % i) for i in range(2)]
        w_sems = [sem("s_w%d" % i) for i in range(2)]
        s_wproj = sem("s_wproj")
        bp_sems = [sem("s_bp%d" % i) for i in range(2)]

        xb_sems = [sem("s_xb%d" % t) for t in range(NT)]

        def load_x(s, t):
            r0 = s * T + t * 128

            def fa(e, t=t, r0=r0):
                return [e.dma_start(out=xs[t][:, 0:768], in_=x_d[r0:r0 + 128, 0:768]),
                        e.dma_start(out=xs[t][:, 768:1536], in_=x_d[r0:r0 + 128, 768:1536])]
            op("sp", fa, writes=[("xs", t, 0), ("xs", t, 1), ("xs", t, 2)], dma_sem=x_sems[t], n_dma=2)

            def fb(e, t=t, r0=r0):
                return [e.dma_start(out=xs[t][:, 1536:2048], in_=x_d[r0:r0 + 128, 1536:2048])]
            op("sp", fb, writes=[("xs", t, 3)], dma_sem=xb_sems[t], n_dma=1)

        for t in range(NT):
            load_x(0, t)
        for c0 in (K0, 0, 512, GA0, GA0 + 512):
            cast("in", win_d, winb_d, c0, c0 + 512, D)
        for half in range(2):
            for base in (CC0, CH0, CB0, GC0):
                cast("in", win_d, winb_d, base + half * 512, base + half * 512 + 512, D)
        for c0 in range(0, D, 512):
            cast("out", wout_d, woutb_d, c0, c0 + 512, D)
        cast("proj", wproj_d, wprojb_d, 0, D, PLE)
        for c0 in range(0, D, 512):
            cast("gate", wgate_d, wgateb_d, c0, c0 + 512, D)
        need_upto = [0, 1, 2, 3, 4, 8, 8, 8, 8, 12, 12, 12, 12, 13, 14, 15, 16, 18, 19, 20, 21]
        issue_casts(1)

        def cast_key_for(name, c):
            blk = (c // 512) * 512 if name != "proj" else 0
            if name == "in" and c >= CB0:
                base = CB0 + ((c - CB0) // 1024) * 1024
                blk = base + ((c - base) // 512) * 512
            return cast_keys[(name, blk)]

        op("pool", lambda e: e.memset(neghalf[:], -0.5), writes=["neghalf"])
        op("pool", lambda e: e.memset(ucarry[:], 0.0), writes=["ucarry"])
        for i in range(5):
            op("pool", lambda e, i=i: e.memset(vr[i][:], 1.0), writes=[("vr", i)])
        op("dve", lambda e: e.tensor_scalar(out=omr[:], in0=rotm[:], scalar1=-1.0, scalar2=1.0,
                                            op0=ALU.mult, op1=ALU.add), reads=["const"], writes=["omr"])
        op("dve", lambda e: e.scalar_tensor_tensor(out=qvec[:], in0=gqc[:], scalar=omr[:, 0:1], in1=rotm[:],
                                                   op0=ALU.mult, op1=ALU.add), reads=["const", "omr"],
           writes=["qvec"])
        op("dve", lambda e: e.scalar_tensor_tensor(out=kvec[:], in0=gkc[:], scalar=omr[:, 0:1], in1=rotm[:],
                                                   op0=ALU.mult, op1=ALU.add), reads=["const", "omr"],
           writes=["kvec"])
        for pre, gb in (("q", gqb), ("k", gkb)):
            for nm, tab, g0 in ((pre + "CA", cost, 0), (pre + "SA", sint, 8), (pre + "CB", cost, 8),
                                (pre + "SB", sint, 0)):
                def f(e, nm=nm, tab=tab, g0=g0, gb=gb):
                    return e.tensor_tensor(
                        out=rtab[nm][:].rearrange("p (b f) -> p b f", b=32),
                        in0=tab[:].rearrange("p (b f) -> p b f", b=32),
                        in1=gb[:, g0:g0 + 8].unsqueeze(1).broadcast_to([128, 32, 8]), op=ALU.mult)
                op("dve", f, reads=["const"], writes=["rtab"])
        op("act", lambda e: e.activation(out=es2[:], in_=sinkb[:], func=AF.Exp), reads=["const"], writes=["es2"])
        op("dve", lambda e: e.tensor_scalar(out=es2[:], in0=es2[:], scalar1=2.0, scalar2=None, op0=ALU.mult),
           reads=["es2"], writes=["es2"])
        op("dve", lambda e: e.tensor_scalar(out=cw[:], in0=cw[:], scalar1=0.5, scalar2=None, op0=ALU.mult),
           reads=["const"], writes=["cw"])

        chunk_plan = []
        for c0 in (K0, 0, 512, GA0, GA0 + 512):
            chunk_plan.append(("in", "tok", c0))
        for j in range(8):
            chunk_plan.append(("in", "conv", j))
        for c0 in range(0, D, 512):
            chunk_plan.append(("out", "tok", c0))
        for c0 in range(0, D, 512):
            chunk_plan.append(("gate", "tok", c0))
        NCH = len(chunk_plan)
        wsrc = {"in": winb_d, "out": woutb_d, "gate": wgateb_d}
        state = {"issued": 0}

        def issue_chunk(gidx):
            if gidx >= NS * NCH:
                return
            name, kind, col = chunk_plan[gidx % NCH]
            slot = gidx % 2
            src = wsrc[name]
            if kind == "tok":
                def f(e, src=src, col=col, slot=slot):
                    v = src.rearrange("(c p) n -> p c n", p=128)
                    return [e.dma_start(out=w3[slot][:, 0:8, :], in_=v[:, 0:8, col:col + 512]),
                            e.dma_start(out=w3[slot][:, 8:16, :], in_=v[:, 8:16, col:col + 512])]
                op("sp", f, reads=[cast_key_for(name, col)], writes=[("w", slot)], dma_sem=w_sems[slot], n_dma=2)
            else:
                j = col

                def f(e, src=src, j=j, slot=slot):
                    v = src.rearrange("(c p) n -> p c n", p=128)
                    r = []
                    for si, base in enumerate((CC0, CH0, CB0, GC0)):
                        for hh in range(2):
                            r.append(e.dma_start(out=w3[slot][:, hh * 8:(hh + 1) * 8, si * 128:(si + 1) * 128],
                                                 in_=v[:, hh * 8:(hh + 1) * 8, base + j * 128: base + (j + 1) * 128]))
                    return r
                rk = [cast_key_for("in", base + j * 128) for base in (CC0, CH0, CB0, GC0)]
                op("sp", f, reads=rk, writes=[("w", slot)], dma_sem=w_sems[slot], n_dma=8)

        def next_chunk():
            g = state["issued"]
            if g < NCH:
                issue_casts(need_upto[min(g + 2, NCH - 1)])
            if g == 0:
                issue_chunk(0)
            issue_chunk(g + 1)
            state["issued"] = g + 1
            return g % 2

        bank_rot = {"i": 0}

        def zbank():
            b = bank_rot["i"] % 4
            bank_rot["i"] += 1
            return b

        def bk(i):
            return [("bk", i, 0), ("bk", i, 1)]

        TB = 4
        fs_rot = {"i": 0}

        def rmsnorm_transpose(src, src_key, t, gT, dstT3, dst_key, ssc, msc, rsc, tag):
            xb = xn[t % 2]
            xk = ("xn", t % 2)
            kss, kms, krs = (tag, "ss", t), (tag, "ms", t), (tag, "rs", t)
            op("act", lambda e: e.activation(out=xb[:], in_=src[:], func=AF.Square, accum_out=ssc),
               reads=list(src_key), writes=[xk, kss])
            op("pool", lambda e: e.tensor_scalar(out=msc, in0=ssc, scalar1=1.0 / D, scalar2=EPS, op0=ALU.mult,
                                                 op1=ALU.add), reads=[kss], writes=[kms])
            op("pool", lambda e: e.tensor_tensor(out=rsc, in0=msc, in1=neghalf[:, 0:1], op=ALU.pow),
               reads=[kms, "neghalf"], writes=[krs])
            op("act", lambda e: e.activation(out=xb[:], in_=src[:], func=AF.Identity, scale=rsc),
               reads=list(src_key) + [krs], writes=[xk])
            for half in range(2):
                tb = (TB, 5)[half]

                def tr(e, half=half, tb=tb):
                    r = None
                    for k in range(8):
                        c = half * 8 + k
                        r = e.transpose(out=bankb[tb][:, k * 128:(k + 1) * 128], in_=xb[:, c * 128:(c + 1) * 128],
                                        identity=ident[:])
                    return r
                op("pe", tr, reads=[xk, "const"], writes=bk(tb))

                def ev(e, half=half, tb=tb):
                    return e.tensor_tensor(
                        out=dstT3[:, half * 8:(half + 1) * 8, t * 128:(t + 1) * 128],
                        in0=bankb[tb][:, 0:1024].rearrange("p (c k) -> p c k", c=8),
                        in1=gT[:, half * 8:(half + 1) * 8].unsqueeze(2).broadcast_to([128, 8, 128]), op=ALU.mult)
                op("dve", ev, reads=bk(tb) + ["const"], writes=[(dst_key, half, t)])

        def mm_tok(bank, lT3, t, slot, lkeys, kcs=KC, w_ap3=None, wkey=None, start=True, stop=True):
            w_ap3 = w3[slot] if w_ap3 is None else w_ap3
            wkey = ("w", slot) if wkey is None else wkey

            def f(e):
                r = None
                for c in range(kcs):
                    r = e.matmul(banks[bank][:, 0:512], lhsT=lT3[:, c, t * 128:(t + 1) * 128], rhs=w_ap3[:, c, 0:512],
                                 start=(start and c == 0), stop=(stop and c == kcs - 1))
                return r
            op("pe", f, reads=lkeys + [wkey], writes=bk(bank))

        hkeys = lambda t: [("hT", 0, t), ("hT", 1, t)]
        hkeys_all = [("hT", h, t) for h in range(2) for t in range(NT)]

        def rope(src3, H, pre, b, dst_fn, tmp, dst_keys):
            bc = lambda nm: rtab[nm][:, b * 8:(b + 1) * 8].unsqueeze(1).broadcast_to([128, H, 8])
            t1 = tmp[:, 0:H * 8].rearrange("p (h f) -> p h f", h=H)
            t2 = tmp[:, 128:128 + H * 8].rearrange("p (h f) -> p h f", h=H)
            a = src3[:, :, 0:8]
            bb = src3[:, :, 8:16]
            rk = ["rtab", "ropesrc"]
            op("dve", lambda e: e.tensor_tensor(out=t1, in0=a, in1=bc(pre + "CA"), op=ALU.mult), reads=rk, writes=["t1"])
            op("dve", lambda e: e.tensor_tensor(out=t2, in0=bb, in1=bc(pre + "SA"), op=ALU.mult), reads=rk, writes=["t2"])
            op("dve", lambda e: e.tensor_tensor(out=dst_fn(0), in0=t1, in1=t2, op=ALU.subtract),
               reads=["t1", "t2"], writes=dst_keys)
            op("dve", lambda e: e.tensor_tensor(out=t1, in0=bb, in1=bc(pre + "CB"), op=ALU.mult), reads=rk, writes=["t1"])
            op("dve", lambda e: e.tensor_tensor(out=t2, in0=a, in1=bc(pre + "SB"), op=ALU.mult), reads=rk, writes=["t2"])
            op("dve", lambda e: e.tensor_tensor(out=dst_fn(1), in0=t1, in1=t2, op=ALU.add),
               reads=["t1", "t2"], writes=dst_keys)

        def qk_norm(bank, col0, H, ssc, msc, rsc, fsq):
            ncol = H * 64
            op("act", lambda e: e.activation(out=fsq[:, 0:ncol], in_=banks[bank][:, col0:col0 + ncol], func=AF.Square),
               reads=bk(bank), writes=[("fs", id(fsq))])
            op("dve", lambda e: e.tensor_reduce(out=ssc, in_=fsq[:, 0:ncol].rearrange("p (h d) -> p h d", h=H),
                                                axis=AX.X, op=ALU.add), reads=[("fs", id(fsq))],
               writes=[("sm", id(ssc))])
            op("pool", lambda e: e.tensor_scalar(out=msc, in0=ssc, scalar1=1.0 / 64, scalar2=EPS, op0=ALU.mult,
                                                 op1=ALU.add), reads=[("sm", id(ssc))], writes=[("sm", id(msc))])
            op("pool", lambda e: e.tensor_tensor(out=rsc, in0=msc, in1=neghalf[:, 0:H], op=ALU.pow),
               reads=[("sm", id(msc)), "neghalf"], writes=[("sm", id(rsc))])

        ss1 = scol(4); ms1 = scol(4); rs1 = scol(4)
        ss2 = scol(4); ms2 = scol(4); rs2 = scol(4)
        ssk = scol(4); msk = scol(4); rsk = scol(4)
        ssq = [scol(8), scol(8)]; msq = [scol(8), scol(8)]; rsq = [scol(8), scol(8)]
        den = scol(4); rden = scol(4)
        ss3 = scol(16); ss3t = scol(4); ms3 = scol(4); rs3 = scol(4)

        store_ops = []

        def norm_part(src, src_key, t, ssc, msc, rsc, tag):
            xb = xn[t % 2]
            xk = ("xn", t % 2)
            kss, kms, krs = (tag, "ss", t), (tag, "ms", t), (tag, "rs", t)
            op("act", lambda e: e.activation(out=xb[:], in_=src[:], func=AF.Square, accum_out=ssc),
               reads=list(src_key), writes=[xk, kss])
            op("pool", lambda e: e.tensor_scalar(out=msc, in0=ssc, scalar1=1.0 / D, scalar2=EPS, op0=ALU.mult,
                                                 op1=ALU.add), reads=[kss], writes=[kms])
            op("pool", lambda e: e.tensor_tensor(out=rsc, in0=msc, in1=neghalf[:, 0:1], op=ALU.pow),
               reads=[kms, "neghalf"], writes=[krs])
            op("act", lambda e: e.activation(out=xb[:], in_=src[:], func=AF.Identity, scale=rsc),
               reads=list(src_key) + [krs], writes=[xk])

        def trans_part(t, gT, dstT3, dst_key):
            xb = xn[t % 2]
            xk = ("xn", t % 2)
            for half in range(2):
                tb = (TB, 5)[half]

                def tr(e, half=half, tb=tb):
                    r = None
                    for k in range(8):
                        c = half * 8 + k
                        r = e.transpose(out=bankb[tb][:, k * 128:(k + 1) * 128], in_=xb[:, c * 128:(c + 1) * 128],
                                        identity=ident[:])
                    return r
                op("pe", tr, reads=[xk, "const"], writes=bk(tb))

                def ev(e, half=half, tb=tb):
                    return e.tensor_tensor(
                        out=dstT3[:, half * 8:(half + 1) * 8, t * 128:(t + 1) * 128],
                        in0=bankb[tb][:, 0:1024].rearrange("p (c k) -> p c k", c=8),
                        in1=gT[:, half * 8:(half + 1) * 8].unsqueeze(2).broadcast_to([128, 8, 128]), op=ALU.mult)
                op("dve", ev, reads=bk(tb) + ["const"], writes=[(dst_key, half, t)])

        gidx = {"i": 0}

        def kv_M(s, t, slot):
            b = s * NT + t
            ring = b % 5
            gi = gidx["i"] % 4
            gidx["i"] += 1
            buf = knq[gi]
            bkey = ("knq", gi)
            zb = zbank()
            mm_tok(zb, hT3, t, slot, hkeys(t))
            fsq = fsc[fs_rot["i"] % 2]
            fs_rot["i"] += 1
            qk_norm(zb, 0, 4, ssk, msk, rsk, fsq)
            pk = banks[zb][:, 0:256].rearrange("p (h d) -> p h d", h=4)
            kn4 = buf[:].rearrange("p (h a d) -> p h a d", h=4, a=2)
            op("dve", lambda e: e.tensor_tensor(
                out=kn4, in0=pk.unsqueeze(2).broadcast_to([128, 4, 2, 64]),
                in1=rsk.unsqueeze(2).unsqueeze(3).broadcast_to([128, 4, 2, 64]), op=ALU.mult),
               reads=bk(zb) + [("sm", id(rsk))], writes=[bkey])
            kr3 = rope_t[:, 256:320].rearrange("p (h f) -> p h f", h=4)
            op("dve", lambda e: e.tensor_tensor(
                out=kr3, in0=pk[:, :, 0:16], in1=rsk.unsqueeze(2).broadcast_to([128, 4, 16]), op=ALU.mult),
               reads=bk(zb) + [("sm", id(rsk))], writes=["ropesrc"])
            kro = rope_t[:, 320:384].rearrange("p (h f) -> p h f", h=4)
            rope(kr3, 4, "k", b, lambda lo: kro[:, :, lo * 8:(lo + 1) * 8], rope_t, ["ropedst"])
            op("dve", lambda e: e.tensor_copy(
                out=kn4[:, :, :, 0:16], in_=kro.unsqueeze(2).broadcast_to([128, 4, 2, 16])),
               reads=["ropedst"], writes=[bkey])
            op("act", lambda e: e.activation(
                out=vr[ring][:].rearrange("p (h d) -> p h d", h=4)[:, :, 0:64],
                in_=banks[zb][:, 256:512].rearrange("p (h d) -> p h d", h=4), func=AF.Copy),
               reads=bk(zb), writes=[("vr", ring)])

            def X():
                def ktr(e):
                    r = None
                    for h in range(4):
                        r = e.transpose(out=bankb[TB][:, h * 128:(h + 1) * 128], in_=buf[:, h * 128:(h + 1) * 128],
                                        identity=ident[:])
                    return r
                op("pe", ktr, reads=[bkey, "const"], writes=bk(TB))
                op("act", lambda e: e.activation(out=kTr[ring][:], in_=bankb[TB][:, 0:512],
                                                 func=AF.Identity, scale=kvec[:, 0:1]),
                   reads=bk(TB) + ["kvec"], writes=[("kTr", ring)])
            return X

        def q_M(s, t, qc, slot):
            b = s * NT + t
            gi = gidx["i"] % 4
            gidx["i"] += 1
            buf = knq[gi]
            bkey = ("knq", gi)
            zb = zbank()
            mm_tok(zb, hT3, t, slot, hkeys(t))
            fsq = fsc[fs_rot["i"] % 2]
            fs_rot["i"] += 1
            rq = rsq[qc]
            qk_norm(zb, 0, 8, ssq[qc], msq[qc], rq, fsq)
            pq = banks[zb][:, 0:512].rearrange("p (h d) -> p h d", h=8)
            qn3 = buf[:].rearrange("p (h d) -> p h d", h=8)
            op("dve", lambda e: e.tensor_tensor(
                out=qn3, in0=pq, in1=rq.unsqueeze(2).broadcast_to([128, 8, 64]), op=ALU.mult),
               reads=bk(zb) + [("sm", id(rq))], writes=[bkey])
            qr3 = rope_t[:, 384:512].rearrange("p (h f) -> p h f", h=8)
            op("dve", lambda e: e.tensor_tensor(
                out=qr3, in0=pq[:, :, 0:16], in1=rq.unsqueeze(2).broadcast_to([128, 8, 16]), op=ALU.mult),
               reads=bk(zb) + [("sm", id(rq))], writes=["ropesrc"])
            rope(qr3, 8, "q", b, lambda lo: qn3[:, :, lo * 8:(lo + 1) * 8], rope_t, [bkey])

            def X():
                def qtr(e):
                    r = None
                    for k in range(4):
                        r = e.transpose(out=bankb[TB][:, k * 128:(k + 1) * 128],
                                        in_=buf[:, k * 128:(k + 1) * 128], identity=ident[:])
                    return r
                op("pe", qtr, reads=[bkey, "const"], writes=bk(TB))
                op("act", lambda e: e.activation(
                    out=qT[t][:, qc * 512:(qc + 1) * 512], in_=bankb[TB][:, 0:512], func=AF.Identity,
                    scale=qvec[:, 0:1]), reads=bk(TB) + ["qvec"], writes=[("qT", t, qc)])
            return X

        def g_M(t, gc, slot):
            zb = zbank()
            mm_tok(zb, hT3, t, slot, hkeys(t))
            th = fsc[2 + fs_rot["i"] % 2]
            fs_rot["i"] += 1
            op("act", lambda e: e.activation(out=th[:, 0:512], in_=banks[zb][:, 0:512],
                                             func=AF.Tanh, scale=0.5),
               reads=bk(zb), writes=[("fs", id(th))])
            op("dve", lambda e: e.scalar_tensor_tensor(
                out=sgl[t][:, gc * 512:(gc + 1) * 512], in0=th[:, 0:512], scalar=1.0, in1=banks[zb][:, 0:512],
                op0=ALU.add, op1=ALU.mult), reads=bk(zb) + [("fs", id(th))], writes=[("sgl", t, gc)])

        def att_sc(s, t, h):
            b = s * NT + t
            rc, rp = b % 5, (b - 1) % 5
            first = (b == 0)

            def sc(e):
                r = None
                for kb, ring in ((0, rp), (1, rc)):
                    if first and kb == 0:
                        continue
                    for half, bank in ((0, 5), (1, 6)):
                        r0 = half * 64
                        r = e.matmul(banks[bank][:, kb * 256:(kb + 1) * 256],
                                     lhsT=kTr[ring][r0:r0 + 64, h * 128:(h + 1) * 128],
                                     rhs=qT[t][r0:r0 + 64, h * 256:(h + 1) * 256], start=True, stop=True)
                return r
            rd = [("kTr", rc), ("qT", t, h // 2)] + ([] if first else [("kTr", rp)])
            op("pe", sc, reads=rd, writes=bk(5) + bk(6))
            c0 = 256 if first else 0
            for half, bank in ((0, 5), (1, 6)):
                pb_i = (2 * h + half) % 4
                pbuf = Pb[pb_i]
                op("act", lambda e, pbuf=pbuf, bank=bank: e.activation(
                    out=pbuf[:, c0:512], in_=banks[bank][:, c0:512], func=AF.Exp, scale=0.125),
                   reads=bk(bank), writes=[("Pb", pb_i)])
                op("dve", lambda e, pbuf=pbuf: e.tensor_tensor(
                    out=pbuf[:, c0:512], in0=pbuf[:, c0:512], in1=maskt[:, c0:512], op=ALU.mult),
                   reads=[("Pb", pb_i), "const"], writes=[("Pb", pb_i)])

        def att_pv(s, t, h):
            b = s * NT + t
            rc, rp = b % 5, (b - 1) % 5
            first = (b == 0)
            mt = mixtok[t % 2]
            pbs = [Pb[(2 * h) % 4], Pb[(2 * h + 1) % 4]]

            def pv(e):
                r = None
                for j in range(2):
                    for half in range(2):
                        hl = 2 * j + half
                        for kb, ring in ((0, rp), (1, rc)):
                            if first and kb == 0:
                                continue
                            r = e.matmul(banks[7][:, hl * 65:(hl + 1) * 65],
                                         lhsT=pbs[half][:, kb * 256 + j * 128: kb * 256 + (j + 1) * 128],
                                         rhs=vr[ring][:, h * 65:(h + 1) * 65],
                                         start=(kb == 0 or first), stop=(kb == 1))
                return r
            rd = [("Pb", (2 * h) % 4), ("Pb", (2 * h + 1) % 4), ("vr", rc)] + ([] if first else [("vr", rp)])
            op("pe", pv, reads=rd, writes=bk(7))
            po3 = banks[7][:, 0:260].rearrange("p (h d) -> p h d", h=4)
            op("dve", lambda e: e.scalar_tensor_tensor(
                out=den, in0=po3[:, :, 64], scalar=2.0, in1=es2[:, 4 * h:4 * h + 4], op0=ALU.mult,
                op1=ALU.add), reads=bk(7) + ["es2"], writes=["den"])
            op("dve", lambda e: e.reciprocal(out=rden, in_=den), reads=["den"], writes=["rden"])
            tmp = fsc[h % 2]
            op("dve", lambda e: e.tensor_tensor(
                out=tmp[:, 0:256].rearrange("p (h d) -> p h d", h=4), in0=po3[:, :, 0:64],
                in1=rden.unsqueeze(2).broadcast_to([128, 4, 64]), op=ALU.mult),
               reads=bk(7) + ["rden"], writes=[("fs", id(tmp))])
            op("pool", lambda e: e.tensor_tensor(
                out=mt[:, h * 256:(h + 1) * 256], in0=tmp[:, 0:256], in1=sgl[t][:, h * 256:(h + 1) * 256],
                op=ALU.mult), reads=[("fs", id(tmp)), ("sgl", t, h // 2)], writes=[("mixtok", t % 2)])

        def att_fin(t):
            mt = mixtok[t % 2]

            def mtr(e):
                r = None
                for k in range(8):
                    r = e.transpose(out=bankb[TB][:, k * 128:(k + 1) * 128], in_=mt[:, k * 128:(k + 1) * 128],
                                    identity=ident[:])
                return r
            op("pe", mtr, reads=[("mixtok", t % 2), "const"], writes=bk(TB))
            op("act", lambda e: e.activation(
                out=mixT3[:, 0:8, t * 128:(t + 1) * 128],
                in_=bankb[TB][:, 0:1024].rearrange("p (c k) -> p c k", c=8), func=AF.Copy),
               reads=bk(TB), writes=[("mixT", "a", t)])

        def conv_seg(slot, si, bank):
            def mmseg(e):
                r = None
                for c in range(KC):
                    r = e.matmul(banks[bank][:, 0:512], lhsT=w3[slot][:, c, si * 128:(si + 1) * 128],
                                 rhs=hT3[:, c, 0:512], start=(c == 0), stop=(c == KC - 1))
                return r
            op("pe", mmseg, reads=hkeys_all + [("w", slot)], writes=bk(bank))

        def conv_u(j):
            ch = fsc[j % 2]
            u = fsc[4 + j % 2]
            op("act", lambda e: e.activation(out=ch[:, 0:512], in_=banks[1][:, 0:512], func=AF.Copy),
               reads=bk(1), writes=[("fs", id(ch))])
            op("dve", lambda e: e.tensor_copy(out=u[:, 0:2], in_=ucarry[:, 2 * j:2 * j + 2]),
               reads=["ucarry%d" % j], writes=[("fs", id(u))])
            op("dve", lambda e: e.tensor_tensor(out=u[:, 2:514], in0=banks[0][:, 0:512], in1=ch[:, 0:512],
                                                op=ALU.mult), reads=bk(0) + [("fs", id(ch))],
               writes=[("fs", id(u))])
            op("dve", lambda e: e.tensor_copy(out=ucarry[:, 2 * j:2 * j + 2], in_=u[:, 512:514]),
               reads=[("fs", id(u))], writes=["ucarry%d" % j])

        def conv_fin(j):
            u = fsc[4 + j % 2]
            acc = fsc[6 + j % 2]
            th = fsc[2 + j % 2]
            bcb, bg = 2, 3
            op("dve", lambda e: e.tensor_scalar(out=acc[:, 0:512], in0=u[:, 2:514], scalar1=cw[:, 3 * j + 2:3 * j + 3],
                                                 scalar2=None, op0=ALU.mult), reads=[("fs", id(u)), "cw"],
               writes=[("fs", id(acc))])
            op("dve", lambda e: e.scalar_tensor_tensor(out=acc[:, 0:512], in0=u[:, 1:513],
                                                        scalar=cw[:, 3 * j + 1:3 * j + 2], in1=acc[:, 0:512],
                                                        op0=ALU.mult, op1=ALU.add),
               reads=[("fs", id(u)), "cw", ("fs", id(acc))], writes=[("fs", id(acc))])
            op("dve", lambda e: e.scalar_tensor_tensor(out=acc[:, 0:512], in0=u[:, 0:512],
                                                        scalar=cw[:, 3 * j:3 * j + 1], in1=acc[:, 0:512],
                                                        op0=ALU.mult, op1=ALU.add),
               reads=[("fs", id(u)), "cw", ("fs", id(acc))], writes=[("fs", id(acc))])
            op("dve", lambda e: e.tensor_tensor(out=acc[:, 0:512], in0=acc[:, 0:512], in1=banks[bcb][:, 0:512],
                                                op=ALU.mult), reads=bk(bcb) + [("fs", id(acc))],
               writes=[("fs", id(acc))])
            op("act", lambda e: e.activation(out=th[:, 0:512], in_=banks[bg][:, 0:512], func=AF.Tanh, scale=0.5),
               reads=bk(bg), writes=[("fs", id(th))])
            op("dve", lambda e: e.scalar_tensor_tensor(out=th[:, 0:512], in0=th[:, 0:512], scalar=1.0,
                                                       in1=banks[bg][:, 0:512], op0=ALU.add, op1=ALU.mult),
               reads=bk(bg) + [("fs", id(th))], writes=[("fs", id(th))])
            op("dve", lambda e: e.tensor_tensor(out=mixT3[:, 8 + j, :], in0=acc[:, 0:512], in1=th[:, 0:512],
                                                op=ALU.mult), reads=[("fs", id(acc)), ("fs", id(th))],
               writes=[("mixT", "c", j)])

        def p_load(s, t):
            r0 = s * T + t * 128
            pi = t % 2
            op("pool", lambda e: [e.dma_start(out=psb[pi][:], in_=p_d[r0:r0 + 128, :])],
               writes=[("psb", pi)], dma_sem=p_sems[pi])
            op("act", lambda e: e.activation(out=pbf[t][:], in_=psb[pi][:], func=AF.Copy),
               reads=[("psb", pi)], writes=[("pbf", t)])

        def p_trans(t):
            def ptr(e):
                r = None
                for k in range(2):
                    r = e.transpose(out=bankb[6][:, k * 128:(k + 1) * 128], in_=pbf[t][:, k * 128:(k + 1) * 128],
                                    identity=ident[:])
                return r
            op("pe", ptr, reads=[("pbf", t), "const"], writes=bk(6))
            op("act", lambda e: e.activation(
                out=pT3[:, :, t * 128:(t + 1) * 128],
                in_=bankb[6][:, 0:256].rearrange("p (c k) -> p c k", c=2), func=AF.Copy),
               reads=bk(6), writes=[("pT", t)])

        def e1_stats(t):
            for n in range(4):
                zb = zbank()
                mm_tok(zb, pT3, t, None, [("pT", t)], kcs=2, w_ap3=wproj3[:, :, n * 512:(n + 1) * 512],
                       wkey="wproj")
                jk = fsc[fs_rot["i"] % 2]
                fs_rot["i"] += 1
                op("act", lambda e, jk=jk, zb=zb, n=n: e.activation(
                    out=jk[:, 0:512], in_=banks[zb][:, 0:512], func=AF.Square,
                    accum_out=ss3[:, t * 4 + n:t * 4 + n + 1]), reads=bk(zb),
                   writes=[("fs", id(jk)), ("ss3", t)])
            op("dve", lambda e: e.tensor_reduce(out=ss3t[:, t:t + 1], in_=ss3[:, t * 4:t * 4 + 4],
                                                axis=AX.X, op=ALU.add), reads=[("ss3", t)],
               writes=[("ss3t", t)])
            op("pool", lambda e: e.tensor_scalar(out=ms3[:, t:t + 1], in0=ss3t[:, t:t + 1],
                                                 scalar1=1.0 / D, scalar2=EPS, op0=ALU.mult, op1=ALU.add),
               reads=[("ss3t", t)], writes=[("ms3", t)])
            op("pool", lambda e: e.tensor_tensor(out=rs3[:, t:t + 1], in0=ms3[:, t:t + 1],
                                                 in1=neghalf[:, 0:1], op=ALU.pow),
               reads=[("ms3", t), "neghalf"], writes=[("rs3", t)])
            op("pool", lambda e: e.tensor_scalar(out=rs3[:, t:t + 1], in0=rs3[:, t:t + 1], scalar1=0.5,
                                                 scalar2=None, op0=ALU.mult),
               reads=[("rs3", t)], writes=[("rs3", t)])

        for s in range(NS):
            nA = lambda t: norm_part(xs[t], [("xs", t, n_) for n_ in range(4)], t, ss1[:, t:t + 1], ms1[:, t:t + 1], rs1[:, t:t + 1], "n1")
            if s == 0:
                nA(0)
                nA(1)
            pend = []
            LAG = 3
            slot = next_chunk()
            trans_part(0, g1T, hT3, "hT")
            nA(2)
            trans_part(1, g1T, hT3, "hT")
            nA(3)
            for t in range(NT):
                pend.append(kv_M(s, t, slot))
                if len(pend) > LAG:
                    pend.pop(0)()
                if t + 2 < NT:
                    trans_part(t + 2, g1T, hT3, "hT")
            for qc in range(2):
                slot = next_chunk()
                for t in range(NT):
                    pend.append(q_M(s, t, qc, slot))
                    if len(pend) > LAG:
                        pend.pop(0)()
            for gc in range(2):
                slot = next_chunk()
                for t in range(NT):
                    g_M(t, gc, slot)
                    if pend:
                        pend.pop(0)()
            while pend:
                pend.pop(0)()

            for t in range(NT):
                p_load(s, t)
            for j in range(8):
                slot = next_chunk()
                t = j // 2
                h0 = 2 * (j % 2)
                conv_seg(slot, 0, 0)
                if j % 2 == 0 and j > 0:
                    att_fin(t - 1)
                att_sc(s, t, h0)
                conv_seg(slot, 1, 1)
                conv_u(j)
                att_pv(s, t, h0)
                att_sc(s, t, h0 + 1)
                conv_seg(slot, 2, 2)
                conv_seg(slot, 3, 3)
                att_pv(s, t, h0 + 1)
                conv_fin(j)
            att_fin(NT - 1)

            mkeys_all = [("mixT", "c", j) for j in range(8)]
            for n in range(4):
                slot = next_chunk()
                for t in range(NT):
                    zb = zbank()
                    mm_tok(zb, mixT3, t, slot, [("mixT", "a", t)] + mkeys_all)
                    op("dve", lambda e, t=t, n=n, zb=zb: e.tensor_tensor(
                        out=xs[t][:, n * 512:(n + 1) * 512], in0=xs[t][:, n * 512:(n + 1) * 512],
                        in1=banks[zb][:, 0:512], op=ALU.add), reads=bk(zb) + [("xs", t, n)], writes=[("xs", t, n)])
                    if n == 0:
                        p_trans(t)
                    if n == 3 and t < 2:
                        pass

            if s == 0:
                issue_casts(17)
                op("sp", lambda e: [e.dma_start(out=wproj3, in_=wprojb_d.rearrange("(c p) n -> p c n", p=128))],
                   reads=[cast_keys[("proj", 0)]], writes=["wproj"], dma_sem=s_wproj)
            nD = lambda t: norm_part(xs[t], [("xs", t, n_) for n_ in range(4)], t, ss2[:, t:t + 1], ms2[:, t:t + 1], rs2[:, t:t + 1], "n2")
            nD(0)
            nD(1)
            e1_stats(0)
            e1_stats(1)
            trans_part(0, g2T, hT3, "hT")
            nD(2)
            e1_stats(2)
            trans_part(1, g2T, hT3, "hT")
            nD(3)
            e1_stats(3)
            trans_part(2, g2T, hT3, "hT")
            trans_part(3, g2T, hT3, "hT")

            def bias_load(n):
                bi = n % 2
                op("sp", lambda e: [
                    e.dma_start(out=bch[bi][:], in_=bias_d[:, n * 512:(n + 1) * 512]),
                    e.dma_start(out=pch[bi][:], in_=pg_d[:, n * 512:(n + 1) * 512])],
                   writes=[("bp", bi)], dma_sem=bp_sems[bi], n_dma=2)
            bias_load(0)
            for n in range(4):
                slot = next_chunk()
                bi = n % 2
                if n + 1 < 4:
                    bias_load(n + 1)
                for t in range(NT):
                    zg = zbank()
                    mm_tok(zg, hT3, t, slot, hkeys(t))
                    ze = zbank()
                    mm_tok(ze, pT3, t, None, [("pT", t)], kcs=2, w_ap3=wproj3[:, :, n * 512:(n + 1) * 512],
                           wkey="wproj")
                    gp = fsc[fs_rot["i"] % 2]
                    ge = fsc[4 + fs_rot["i"] % 2]
                    fs_rot["i"] += 1
                    op("dve", lambda e, gp=gp, zg=zg, bi=bi: e.tensor_tensor(
                        out=gp[:, 0:512], in0=banks[zg][:, 0:512], in1=bch[bi][:], op=ALU.add),
                       reads=bk(zg) + [("bp", bi)], writes=[("fs", id(gp))])
                    op("act", lambda e, gp=gp: e.activation(out=gp[:, 0:512], in_=gp[:, 0:512], func=AF.Tanh, scale=0.5),
                       reads=[("fs", id(gp))], writes=[("fs", id(gp))])
                    op("dve", lambda e, gp=gp, ge=ge, ze=ze: e.scalar_tensor_tensor(
                        out=ge[:, 0:512], in0=gp[:, 0:512], scalar=1.0, in1=banks[ze][:, 0:512], op0=ALU.add,
                        op1=ALU.mult), reads=bk(ze) + [("fs", id(gp))], writes=[("fs", id(ge))])
                    op("dve", lambda e, ge=ge, bi=bi: e.tensor_tensor(
                        out=ge[:, 0:512], in0=ge[:, 0:512], in1=pch[bi][:], op=ALU.mult),
                       reads=[("fs", id(ge)), ("bp", bi)], writes=[("fs", id(ge))])
                    op("dve", lambda e, ge=ge, t=t, n=n: e.scalar_tensor_tensor(
                        out=xs[t][:, n * 512:(n + 1) * 512], in0=ge[:, 0:512], scalar=rs3[:, t:t + 1],
                        in1=xs[t][:, n * 512:(n + 1) * 512], op0=ALU.mult, op1=ALU.add),
                       reads=[("fs", id(ge)), ("rs3", t), ("xs", t, n)], writes=[("xs", t, n)])
                    r0 = s * T + t * 128

                    def st(e, t=t, r0=r0, n=n):
                        return [e.dma_start(out=y_d[r0:r0 + 128, n * 512:(n + 1) * 512],
                                            in_=xs[t][:, n * 512:(n + 1) * 512])]
                    store_ops.append(op("sp", st, reads=[("xs", t, n)], dma_sem=y_sems[t], n_dma=1))
                    if n == 3:
                        if s + 1 < NS:
                            load_x(s + 1, t)
                            nAn = lambda tt: norm_part(xs[tt], [("xs", tt, n_) for n_ in range(4)], tt, ss1[:, tt:tt + 1], ms1[:, tt:tt + 1],
                                                       rs1[:, tt:tt + 1], "n1")
                            if t == 2:
                                nAn(0)
                            if t == 3:
                                nAn(1)
        fin = op("pool", None)
        fin.deps = list(store_ops[-4 * NT:])
        for o in fin.deps:
            o.has_dep = True

        with nc.Block() as block:
            Sd.emit(block, qs, dict(pe="tensor", act="scalar", dve="vector", pool="gpsimd", sp="sync"))
    return nc


_CACHE = {}


def _consts():
    half = 8
    inv_freq = np.power(np.float32(500000.0), -np.arange(half, dtype=np.float32) * np.float32(2.0) / np.float32(16))
    pos = np.arange(S, dtype=np.float32)
    ang = pos[:, None] * inv_freq[None, :]
    cos = np.cos(ang).astype(np.float32)
    sin = np.sin(ang).astype(np.float32)
    cost = np.ascontiguousarray(cos.reshape(32, 128, 8).transpose(1, 0, 2)).reshape(128, 256)
    sint = np.ascontiguousarray(sin.reshape(32, 128, 8).transpose(1, 0, 2)).reshape(128, 256)
    ident = np.eye(128, dtype=np.float32).astype(ml_dtypes.bfloat16)
    j = np.arange(128)[:, None]
    i = np.arange(128)[None, :]
    prev = (i < j).astype(np.float32)
    cur = (i >= j).astype(np.float32)
    mask = np.concatenate([prev, prev, cur, cur], axis=1).astype(ml_dtypes.bfloat16)
    rotm = ((np.arange(128) % 64) < 16).astype(np.float32).reshape(128, 1)
    return cost, sint, ident, mask, rotm


def kernel(x, p, norm_gain, w_in, q_norm_gain, k_norm_gain, attn_sinks, conv_w, w_out,
           ple_gate_norm_gain, w_ple_gate, b_ple_gate, w_ple_proj, ple_norm_gain):
    f = lambda a: np.ascontiguousarray(np.asarray(a, dtype=np.float32))
    x = f(x); p = f(p)
    cost, sint, ident, mask, rotm = _consts()
    ng = f(norm_gain)[0]
    g2 = f(ple_gate_norm_gain)[0]
    gq = f(q_norm_gain)[0]
    gk = f(k_norm_gain)[0]
    cwv = f(conv_w)[0]
    shared = dict(
        w_in=f(w_in)[0], w_out=f(w_out)[0], w_gate=f(w_ple_gate)[0], w_proj=f(w_ple_proj)[0],
        g1T=np.ascontiguousarray(ng.reshape(KC, 128).T), g2T=np.ascontiguousarray(g2.reshape(KC, 128).T),
        gqb=np.ascontiguousarray(np.broadcast_to(gq[None, :], (128, 64))),
        gkb=np.ascontiguousarray(np.broadcast_to(gk[None, :], (128, 64))),
        gqc=np.ascontiguousarray(np.tile(gq, 2).reshape(128, 1)),
        gkc=np.ascontiguousarray(np.tile(gk, 2).reshape(128, 1)),
        rotm=rotm,
        sinkb=np.ascontiguousarray(np.broadcast_to(f(attn_sinks)[0][None, :], (128, 16))),
        cw=np.ascontiguousarray(cwv.reshape(3, 8, 128).transpose(2, 1, 0)).reshape(128, 24),
        biasb=np.ascontiguousarray(np.broadcast_to(f(b_ple_gate)[0][None, :], (128, D))),
        pgb=np.ascontiguousarray(np.broadcast_to(f(ple_norm_gain)[0][None, :], (128, D))),
        cost=cost, sint=sint, ident=ident, maskt=mask,
    )
    if "nc" not in _CACHE:
        _CACHE["nc"] = build_program()
    nc = _CACHE["nc"]
    in_maps = []
    for c in range(N_CORES):
        m = dict(shared)
        m["x"] = x[c]
        m["p"] = p[0, c]
        in_maps.append(m)
    res = run_bass_kernel_spmd(nc, in_maps, core_ids=list(range(N_CORES)))
    out = np.stack([np.asarray(r["y"], dtype=np.float32) for r in res.results], axis=0)
    return out
```

```python
import numpy as np
import ml_dtypes
from contextlib import ExitStack
import concourse.bass as bass
import concourse.mybir as mybir
from concourse.bass_utils import run_bass_kernel_spmd

F32 = mybir.dt.float32
BF16 = mybir.dt.bfloat16
ALU = mybir.AluOpType
AF = mybir.ActivationFunctionType
AX = mybir.AxisListType

N_CORES = 8
D = 2048
S = 4096
T = 512
NT = 4
NS = S // T
KC = 16
PLE = 256
IN_W = 6656
EPS = 1e-6
Q0, K0, V0, GA0, CB0, CC0, CH0, GC0 = 0, 1024, 1280, 1536, 2560, 3584, 4608, 5632


class _Op:
    __slots__ = ("q", "fn", "deps", "has_dep", "ms", "is_dma", "sem", "val")


class Sched:
    QUEUES = ("pe", "act", "dve", "pool", "sp")

    def __init__(self):
        self.queues = {q: [] for q in self.QUEUES}
        self.last_w = {}
        self.readers = {}
        self.dma_vals = {}

    def op(self, q, fn, reads=(), writes=(), dma_sem=None, n_dma=1):
        o = _Op()
        o.q = q
        o.fn = fn
        o.has_dep = False
        o.ms = None
        o.is_dma = dma_sem is not None
        o.sem = dma_sem
        if o.is_dma:
            v = self.dma_vals.get(dma_sem, 0) + 16 * n_dma
            self.dma_vals[dma_sem] = v
            o.val = v
        else:
            o.val = None
        def _canon(r):
            return ("bk", r[1]) if (isinstance(r, tuple) and r[0] == "bk") else r
        reads = [_canon(r) for r in reads]
        writes = [_canon(r) for r in writes]
        writes = writes + [r for r in reads if isinstance(r, tuple) and r[0] == "bk"]
        reads = [r for r in reads if not (isinstance(r, tuple) and r[0] == "bk")]
        deps = []
        raw = set()
        for r in reads:
            w = self.last_w.get(r)
            if w is not None:
                deps.append(w)
                raw.add(id(w))
        for r in writes:
            w = self.last_w.get(r)
            if w is not None:
                deps.append(w)
            deps.extend(self.readers.get(r, ()))
        od = []
        seen = set()
        for d in deps:
            if id(d) in seen:
                continue
            seen.add(id(d))
            if d.q == q and not d.is_dma and q == "pe":
                continue
            d.has_dep = True
            od.append(d)
        o.deps = od
        for r in reads:
            self.readers.setdefault(r, []).append(o)
        for r in writes:
            self.last_w[r] = o
            self.readers[r] = []
        self.queues[q].append(o)
        return o

    def emit(self, block, qsems, engines):
        for q in self.QUEUES:
            c = 0
            for o in self.queues[q]:
                if o.has_dep and not o.is_dma:
                    c += 1
                    o.ms = c

        def run(q, eng):
            seen = {}
            for o in self.queues[q]:
                need = {}
                for d in o.deps:
                    if d.is_dma:
                        s, v = d.sem, d.val
                    else:
                        s, v = qsems[d.q], d.ms
                    if seen.get(s, 0) >= v:
                        continue
                    if need.get(s, 0) < v:
                        need[s] = v
                for s, v in need.items():
                    eng.wait_ge(s, v)
                    seen[s] = v
                if o.fn is None:
                    continue
                r = o.fn(eng)
                if o.is_dma:
                    for ins in r:
                        ins.then_inc(o.sem, 16)
                elif o.ms is not None:
                    r.then_inc(qsems[q], 1)

        for q in self.QUEUES:
            if not self.queues[q]:
                continue
            dec = getattr(block, engines[q])

            def mk(q=q):
                def _f(eng):
                    run(q, eng)
                return _f
            dec(mk())


def build_program():
    nc = bass.Bass("TRN2", target_bir_lowering=False)
    dram = lambda n, s, d, k: nc.dram_tensor(n, s, d, kind=k).ap()
    x_d = dram("x", [S, D], F32, "ExternalInput")
    p_d = dram("p", [S, PLE], F32, "ExternalInput")
    win_d = dram("w_in", [D, IN_W], F32, "ExternalInput")
    wout_d = dram("w_out", [D, D], F32, "ExternalInput")
    wgate_d = dram("w_gate", [D, D], F32, "ExternalInput")
    wproj_d = dram("w_proj", [PLE, D], F32, "ExternalInput")
    g1T_d = dram("g1T", [128, KC], F32, "ExternalInput")
    g2T_d = dram("g2T", [128, KC], F32, "ExternalInput")
    gqb_d = dram("gqb", [128, 64], F32, "ExternalInput")
    gkb_d = dram("gkb", [128, 64], F32, "ExternalInput")
    gqc_d = dram("gqc", [128, 1], F32, "ExternalInput")
    gkc_d = dram("gkc", [128, 1], F32, "ExternalInput")
    rotm_d = dram("rotm", [128, 1], F32, "ExternalInput")
    sink_d = dram("sinkb", [128, 16], F32, "ExternalInput")
    cw_d = dram("cw", [128, 24], F32, "ExternalInput")
    bias_d = dram("biasb", [128, D], F32, "ExternalInput")
    pg_d = dram("pgb", [128, D], F32, "ExternalInput")
    cos_d = dram("cost", [128, 256], F32, "ExternalInput")
    sin_d = dram("sint", [128, 256], F32, "ExternalInput")
    ident_d = dram("ident", [128, 128], BF16, "ExternalInput")
    mask_d = dram("maskt", [128, 512], BF16, "ExternalInput")
    y_d = dram("y", [S, D], F32, "ExternalOutput")
    winb_d = dram("w_in_bf", [D, IN_W], BF16, "Internal")
    woutb_d = dram("w_out_bf", [D, D], BF16, "Internal")
    wgateb_d = dram("w_gate_bf", [D, D], BF16, "Internal")
    wprojb_d = dram("w_proj_bf", [PLE, D], BF16, "Internal")

    Sd = Sched()
    op = Sd.op
    es = ExitStack()
    with es:
        sb = lambda n, s, d: es.enter_context(nc.sbuf_tensor(n, s, d))
        sem = lambda n: es.enter_context(nc.semaphore(n))
        qs = {q: sem("q_" + q) for q in Sched.QUEUES}
        banks = [es.enter_context(nc.psum_tensor("bank%d" % i, [128, 512], F32)) for i in range(8)]
        bankb = [b.bitcast(BF16) for b in banks]

        xs = [sb("xs%d" % t, [128, D], F32) for t in range(NT)]
        xn = [sb("xn%d" % i, [128, D], BF16) for i in range(2)]
        hT = sb("hT", [128, KC * T], BF16)
        mixT = sb("mixT", [128, KC * T], BF16)
        wsl = [sb("wsl%d" % i, [128, KC * 512], BF16) for i in range(2)]
        qT = [sb("qT%d" % t, [128, 1024], BF16) for t in range(NT)]
        sgl = [sb("sgl%d" % t, [128, 1024], F32) for t in range(NT)]
        kTr = [sb("kTr%d" % i, [128, 512], BF16) for i in range(5)]
        vr = [sb("vr%d" % i, [128, 4 * 65], BF16) for i in range(5)]
        knq = [sb("knq%d" % i, [128, 512], BF16) for i in range(4)]
        Pb = [sb("Pb%d" % i, [128, 512], BF16) for i in range(4)]
        mixtok = [sb("mixtok%d" % i, [128, 1024], BF16) for i in range(2)]
        fsc = [sb("fsc%d" % i, [128, 516], F32) for i in range(8)]
        bch = [sb("bch%d" % i, [128, 512], F32) for i in range(2)]
        pch = [sb("pch%d" % i, [128, 512], F32) for i in range(2)]
        wproj = sb("wproj", [128, 2 * D], BF16)
        pT = sb("pT", [128, 2 * T], BF16)
        psb = [sb("psb%d" % i, [128, PLE], F32) for i in range(2)]
        pbf = [sb("pbf%d" % i, [128, PLE], BF16) for i in range(4)]
        ident = sb("ident_s", [128, 128], BF16)
        maskt = sb("mask_s", [128, 512], BF16)
        g1T = sb("g1T_s", [128, KC], F32)
        g2T = sb("g2T_s", [128, KC], F32)
        gqb = sb("gqb_s", [128, 64], F32)
        gkb = sb("gkb_s", [128, 64], F32)
        qvec = sb("qvec", [128, 1], F32)
        kvec = sb("kvec", [128, 1], F32)
        gqc = sb("gqc_s", [128, 1], F32)
        gkc = sb("gkc_s", [128, 1], F32)
        rotm = sb("rotm_s", [128, 1], F32)
        omr = sb("omr", [128, 1], F32)
        sinkb = sb("sink_s", [128, 16], F32)
        es2 = sb("es2", [128, 16], F32)
        cw = sb("cw_s", [128, 24], F32)
        cost = sb("cos_s", [128, 256], F32)
        sint = sb("sin_s", [128, 256], F32)
        rtab = {}
        for nm in ("qCA", "qSA", "qCB", "qSB", "kCA", "kSA", "kCB", "kSB"):
            rtab[nm] = sb("rt_" + nm, [128, 256], F32)
        neghalf = sb("neghalf", [128, 16], F32)
        small = sb("small", [128, 256], F32)
        ucarry = sb("ucarry", [128, 16], F32)
        rope_t = sb("rope_t", [128, 512], F32)

        _sc = [0]

        def scol(n):
            c = _sc[0]
            _sc[0] += n
            assert _sc[0] <= 256
            return small[:, c:c + n]

        hT3 = hT[:].rearrange("p (c k) -> p c k", c=KC)
        mixT3 = mixT[:].rearrange("p (c k) -> p c k", c=KC)
        w3 = [w[:].rearrange("p (c k) -> p c k", c=KC) for w in wsl]
        wproj3 = wproj[:].rearrange("p (c k) -> p c k", c=2)
        pT3 = pT[:].rearrange("p (c k) -> p c k", c=2)

        s_const = sem("s_const")
        const_list = [(ident, ident_d), (maskt, mask_d), (g1T, g1T_d), (g2T, g2T_d), (gqb, gqb_d),
                      (gkb, gkb_d), (gqc, gqc_d), (gkc, gkc_d), (rotm, rotm_d), (sinkb, sink_d),
                      (cw, cw_d), (cost, cos_d), (sint, sin_d)]

        def ld_const(e):
            return [e.dma_start(out=a[:], in_=b) for a, b in const_list]
        op("sp", ld_const, writes=["const"], dma_sem=s_const, n_dma=len(const_list))

        cast_keys = {}
        cast_list = []
        cast_state = {"n": 0}

        def cast(name, src, dst, c0, c1, r1):
            cast_list.append((name, src, dst, c0, c1, r1))
            cast_keys[(name, c0)] = ("wbf", name, c0)

        def issue_casts(upto):
            while cast_state["n"] <= min(upto, len(cast_list) - 1):
                i = cast_state["n"]
                name, src, dst, c0, c1, r1 = cast_list[i]
                s_ = sem("s_cast_%s_%d" % (name, c0))
                npc = 4 if r1 >= 512 else 1
                rs = r1 // npc

                def f(e, src=src, dst=dst, c0=c0, c1=c1, npc=npc, rs=rs):
                    return [e.dma_start(out=dst[k * rs:(k + 1) * rs, c0:c1], in_=src[k * rs:(k + 1) * rs, c0:c1],
                                        max_dma_last_dim=8192) for k in range(npc)]
                rd = []
                if i >= 2:
                    pn, _, _, pc0, _, _ = cast_list[i - 2]
                    rd = [("wbf", pn, pc0)]
                op("pool", f, reads=rd, writes=[("wbf", name, c0)], dma_sem=s_, n_dma=npc)
                cast_state["n"] = i + 1

        x_sems = [sem("s_x%d" % t) for t in range(NT)]
        y_sems = [sem("s_y%d" % t) for t in range(NT)]
        p_sems = [sem("s_p%d" % i) for i in range(2)]
        w_sems = [sem("s_w%d" % i) for i in range(2)]
        s_wproj = sem("s_wproj")
        bp_sems = [sem("s_bp%d" % i) for i in range(2)]

        xb_sems = [sem("s_xb%d" % t) for t in range(NT)]

        def load_x(s, t):
            r0 = s * T + t * 128

            def fa(e, t=t, r0=r0):
                return [e.dma_start(out=xs[t][:, 0:768], in_=x_d[r0:r0 + 128, 0:768]),
                        e.dma_start(out=xs[t][:, 768:1536], in_=x_d[r0:r0 + 128, 768:1536])]
            op("sp", fa, writes=[("xs", t, 0), ("xs", t, 1), ("xs", t, 2)], dma_sem=x_sems[t], n_dma=2)

            def fb(e, t=t, r0=r0):
                return [e.dma_start(out=xs[t][:, 1536:2048], in_=x_d[r0:r0 + 128, 1536:2048])]
            op("sp", fb, writes=[("xs", t, 3)], dma_sem=xb_sems[t], n_dma=1)

        for t in range(NT):
            load_x(0, t)
        for c0 in (K0, 0, 512, GA0, GA0 + 512):
            cast("in", win_d, winb_d, c0, c0 + 512, D)
        for half in range(2):
            for base in (CC0, CH0, CB0, GC0):
                cast("in", win_d, winb_d, base + half * 512, base + half * 512 + 512, D)
        for c0 in range(0, D, 512):
            cast("out", wout_d, woutb_d, c0, c0 + 512, D)
        cast("proj", wproj_d, wprojb_d, 0, D, PLE)
        for c0 in range(0, D, 512):
            cast("gate", wgate_d, wgateb_d, c0, c0 + 512, D)
        need_upto = [0, 1, 2, 3, 4, 8, 8, 8, 8, 12, 12, 12, 12, 13, 14, 15, 16, 18, 19, 20, 21]
        issue_casts(1)

        def cast_key_for(name, c):
            blk = (c // 512) * 512 if name != "proj" else 0
            if name == "in" and c >= CB0:
                base = CB0 + ((c - CB0) // 1024) * 1024
                blk = base + ((c - base) // 512) * 512
            return cast_keys[(name, blk)]

        op("pool", lambda e: e.memset(neghalf[:], -0.5), writes=["neghalf"])
        op("pool", lambda e: e.memset(ucarry[:], 0.0), writes=["ucarry"])
        for i in range(5):
            op("pool", lambda e, i=i: e.memset(vr[i][:], 1.0), writes=[("vr", i)])
        op("dve", lambda e: e.tensor_scalar(out=omr[:], in0=rotm[:], scalar1=-1.0, scalar2=1.0,
                                            op0=ALU.mult, op1=ALU.add), reads=["const"], writes=["omr"])
        op("dve", lambda e: e.scalar_tensor_tensor(out=qvec[:], in0=gqc[:], scalar=omr[:, 0:1], in1=rotm[:],
                                                   op0=ALU.mult, op1=ALU.add), reads=["const", "omr"],
           writes=["qvec"])
        op("dve", lambda e: e.scalar_tensor_tensor(out=kvec[:], in0=gkc[:], scalar=omr[:, 0:1], in1=rotm[:],
                                                   op0=ALU.mult, op1=ALU.add), reads=["const", "omr"],
           writes=["kvec"])
        for pre, gb in (("q", gqb), ("k", gkb)):
            for nm, tab, g0 in ((pre + "CA", cost, 0), (pre + "SA", sint, 8), (pre + "CB", cost, 8),
                                (pre + "SB", sint, 0)):
                def f(e, nm=nm, tab=tab, g0=g0, gb=gb):
                    return e.tensor_tensor(
                        out=rtab[nm][:].rearrange("p (b f) -> p b f", b=32),
                        in0=tab[:].rearrange("p (b f) -> p b f", b=32),
                        in1=gb[:, g0:g0 + 8].unsqueeze(1).broadcast_to([128, 32, 8]), op=ALU.mult)
                op("dve", f, reads=["const"], writes=["rtab"])
        op("act", lambda e: e.activation(out=es2[:], in_=sinkb[:], func=AF.Exp), reads=["const"], writes=["es2"])
        op("dve", lambda e: e.tensor_scalar(out=es2[:], in0=es2[:], scalar1=2.0, scalar2=None, op0=ALU.mult),
           reads=["es2"], writes=["es2"])
        op("dve", lambda e: e.tensor_scalar(out=cw[:], in0=cw[:], scalar1=0.5, scalar2=None, op0=ALU.mult),
           reads=["const"], writes=["cw"])

        chunk_plan = []
        for c0 in (K0, 0, 512, GA0, GA0 + 512):
            chunk_plan.append(("in", "tok", c0))
        for j in range(8):
            chunk_plan.append(("in", "conv", j))
        for c0 in range(0, D, 512):
            chunk_plan.append(("out", "tok", c0))
        for c0 in range(0, D, 512):
            chunk_plan.append(("gate", "tok", c0))
        NCH = len(chunk_plan)
        wsrc = {"in": winb_d, "out": woutb_d, "gate": wgateb_d}
        state = {"issued": 0}

        def issue_chunk(gidx):
            if gidx >= NS * NCH:
                return
            name, kind, col = chunk_plan[gidx % NCH]
            slot = gidx % 2
            src = wsrc[name]
            if kind == "tok":
                def f(e, src=src, col=col, slot=slot):
                    v = src.rearrange("(c p) n -> p c n", p=128)
                    return [e.dma_start(out=w3[slot][:, 0:8, :], in_=v[:, 0:8, col:col + 512]),
                            e.dma_start(out=w3[slot][:, 8:16, :], in_=v[:, 8:16, col:col + 512])]
                op("sp", f, reads=[cast_key_for(name, col)], writes=[("w", slot)], dma_sem=w_sems[slot], n_dma=2)
            else:
                j = col

                def f(e, src=src, j=j, slot=slot):
                    v = src.rearrange("(c p) n -> p c n", p=128)
                    r = []
                    for si, base in enumerate((CC0, CH0, CB0, GC0)):
                        for hh in range(2):
                            r.append(e.dma_start(out=w3[slot][:, hh * 8:(hh + 1) * 8, si * 128:(si + 1) * 128],
                                                 in_=v[:, hh * 8:(hh + 1) * 8, base + j * 128: base + (j + 1) * 128]))
                    return r
                rk = [cast_key_for("in", base + j * 128) for base in (CC0, CH0, CB0, GC0)]
                op("sp", f, reads=rk, writes=[("w", slot)], dma_sem=w_sems[slot], n_dma=8)

        def next_chunk():
            g = state["issued"]
            if g < NCH:
                issue_casts(need_upto[min(g + 2, NCH - 1)])
            if g == 0:
                issue_chunk(0)
            issue_chunk(g + 1)
            state["issued"] = g + 1
            return g % 2

        bank_rot = {"i": 0}

        def zbank():
            b = bank_rot["i"] % 4
            bank_rot["i"] += 1
            return b

        def bk(i):
            return [("bk", i, 0), ("bk", i, 1)]

        TB = 4
        fs_rot = {"i": 0}

        def rmsnorm_transpose(src, src_key, t, gT, dstT3, dst_key, ssc, msc, rsc, tag):
            xb = xn[t % 2]
            xk = ("xn", t % 2)
            kss, kms, krs = (tag, "ss", t), (tag, "ms", t), (tag, "rs", t)
            op("act", lambda e: e.activation(out=xb[:], in_=src[:], func=AF.Square, accum_out=ssc),
               reads=list(src_key), writes=[xk, kss])
            op("pool", lambda e: e.tensor_scalar(out=msc, in0=ssc, scalar1=1.0 / D, scalar2=EPS, op0=ALU.mult,
                                                 op1=ALU.add), reads=[kss], writes=[kms])
            op("pool", lambda e: e.tensor_tensor(out=rsc, in0=msc, in1=neghalf[:, 0:1], op=ALU.pow),
               reads=[kms, "neghalf"], writes=[krs])
            op("act", lambda e: e.activation(out=xb[:], in_=src[:], func=AF.Identity, scale=rsc),
               reads=list(src_key) + [krs], writes=[xk])
            for half in range(2):
                tb = (TB, 5)[half]

                def tr(e, half=half, tb=tb):
                    r = None
                    for k in range(8):
                        c = half * 8 + k
                        r = e.transpose(out=bankb[tb][:, k * 128:(k + 1) * 128], in_=xb[:, c * 128:(c + 1) * 128],
                                        identity=ident[:])
                    return r
                op("pe", tr, reads=[xk, "const"], writes=bk(tb))

                def ev(e, half=half, tb=tb):
                    return e.tensor_tensor(
                        out=dstT3[:, half * 8:(half + 1) * 8, t * 128:(t + 1) * 128],
                        in0=bankb[tb][:, 0:1024].rearrange("p (c k) -> p c k", c=8),
                        in1=gT[:, half * 8:(half + 1) * 8].unsqueeze(2).broadcast_to([128, 8, 128]), op=ALU.mult)
                op("dve", ev, reads=bk(tb) + ["const"], writes=[(dst_key, half, t)])

        def mm_tok(bank, lT3, t, slot, lkeys, kcs=KC, w_ap3=None, wkey=None, start=True, stop=True):
            w_ap3 = w3[slot] if w_ap3 is None else w_ap3
            wkey = ("w", slot) if wkey is None else wkey

            def f(e):
                r = None
                for c in range(kcs):
                    r = e.matmul(banks[bank][:, 0:512], lhsT=lT3[:, c, t * 128:(t + 1) * 128], rhs=w_ap3[:, c, 0:512],
                                 start=(start and c == 0), stop=(stop and c == kcs - 1))
                return r
            op("pe", f, reads=lkeys + [wkey], writes=bk(bank))

        hkeys = lambda t: [("hT", 0, t), ("hT", 1, t)]
        hkeys_all = [("hT", h, t) for h in range(2) for t in range(NT)]

        def rope(src3, H, pre, b, dst_fn, tmp, dst_keys):
            bc = lambda nm: rtab[nm][:, b * 8:(b + 1) * 8].unsqueeze(1).broadcast_to([128, H, 8])
            t1 = tmp[:, 0:H * 8].rearrange("p (h f) -> p h f", h=H)
            t2 = tmp[:, 128:128 + H * 8].rearrange("p (h f) -> p h f", h=H)
            a = src3[:, :, 0:8]
            bb = src3[:, :, 8:16]
            rk = ["rtab", "ropesrc"]
            op("dve", lambda e: e.tensor_tensor(out=t1, in0=a, in1=bc(pre + "CA"), op=ALU.mult), reads=rk, writes=["t1"])
            op("dve", lambda e: e.tensor_tensor(out=t2, in0=bb, in1=bc(pre + "SA"), op=ALU.mult), reads=rk, writes=["t2"])
            op("dve", lambda e: e.tensor_tensor(out=dst_fn(0), in0=t1, in1=t2, op=ALU.subtract),
               reads=["t1", "t2"], writes=dst_keys)
            op("dve", lambda e: e.tensor_tensor(out=t1, in0=bb, in1=bc(pre + "CB"), op=ALU.mult), reads=rk, writes=["t1"])
            op("dve", lambda e: e.tensor_tensor(out=t2, in0=a, in1=bc(pre + "SB"), op=ALU.mult), reads=rk, writes=["t2"])
            op("dve", lambda e: e.tensor_tensor(out=dst_fn(1), in0=t1, in1=t2, op=ALU.add),
               reads=["t1", "t2"], writes=dst_keys)

        def qk_norm(bank, col0, H, ssc, msc, rsc, fsq):
            ncol = H * 64
            op("act", lambda e: e.activation(out=fsq[:, 0:ncol], in_=banks[bank][:, col0:col0 + ncol], func=AF.Square),
               reads=bk(bank), writes=[("fs", id(fsq))])
            op("dve", lambda e: e.tensor_reduce(out=ssc, in_=fsq[:, 0:ncol].rearrange("p (h d) -> p h d", h=H),
                                                axis=AX.X, op=ALU.add), reads=[("fs", id(fsq))],
               writes=[("sm", id(ssc))])
            op("pool", lambda e: e.tensor_scalar(out=msc, in0=ssc, scalar1=1.0 / 64, scalar2=EPS, op0=ALU.mult,
                                                 op1=ALU.add), reads=[("sm", id(ssc))], writes=[("sm", id(msc))])
            op("pool", lambda e: e.tensor_tensor(out=rsc, in0=msc, in1=neghalf[:, 0:H], op=ALU.pow),
               reads=[("sm", id(msc)), "neghalf"], writes=[("sm", id(rsc))])

        ss1 = scol(4); ms1 = scol(4); rs1 = scol(4)
        ss2 = scol(4); ms2 = scol(4); rs2 = scol(4)
        ssk = scol(4); msk = scol(4); rsk = scol(4)
        ssq = [scol(8), scol(8)]; msq = [scol(8), scol(8)]; rsq = [scol(8), scol(8)]
        den = scol(4); rden = scol(4)
        ss3 = scol(16); ss3t = scol(4); ms3 = scol(4); rs3 = scol(4)

        store_ops = []

        def norm_part(src, src_key, t, ssc, msc, rsc, tag):
            xb = xn[t % 2]
            xk = ("xn", t % 2)
            kss, kms, krs = (tag, "ss", t), (tag, "ms", t), (tag, "rs", t)
            op("act", lambda e: e.activation(out=xb[:], in_=src[:], func=AF.Square, accum_out=ssc),
               reads=list(src_key), writes=[xk, kss])
            op("pool", lambda e: e.tensor_scalar(out=msc, in0=ssc, scalar1=1.0 / D, scalar2=EPS, op0=ALU.mult,
                                                 op1=ALU.add), reads=[kss], writes=[kms])
            op("pool", lambda e: e.tensor_tensor(out=rsc, in0=msc, in1=neghalf[:, 0:1], op=ALU.pow),
               reads=[kms, "neghalf"], writes=[krs])
            op("act", lambda e: e.activation(out=xb[:], in_=src[:], func=AF.Identity, scale=rsc),
               reads=list(src_key) + [krs], writes=[xk])

        def trans_part(t, gT, dstT3, dst_key):
            xb = xn[t % 2]
            xk = ("xn", t % 2)
            for half in range(2):
                tb = (TB, 5)[half]

                def tr(e, half=half, tb=tb):
                    r = None
                    for k in range(8):
                        c = half * 8 + k
                        r = e.transpose(out=bankb[tb][:, k * 128:(k + 1) * 128], in_=xb[:, c * 128:(c + 1) * 128],
                                        identity=ident[:])
                    return r
                op("pe", tr, reads=[xk, "const"], writes=bk(tb))

                def ev(e, half=half, tb=tb):
                    return e.tensor_tensor(
                        out=dstT3[:, half * 8:(half + 1) * 8, t * 128:(t + 1) * 128],
                        in0=bankb[tb][:, 0:1024].rearrange("p (c k) -> p c k", c=8),
                        in1=gT[:, half * 8:(half + 1) * 8].unsqueeze(2).broadcast_to([128, 8, 128]), op=ALU.mult)
                op("dve", ev, reads=bk(tb) + ["const"], writes=[(dst_key, half, t)])

        gidx = {"i": 0}

        def kv_M(s, t, slot):
            b = s * NT + t
            ring = b % 5
            gi = gidx["i"] % 4
            gidx["i"] += 1
            buf = knq[gi]
            bkey = ("knq", gi)
            zb = zbank()
            mm_tok(zb, hT3, t, slot, hkeys(t))
            fsq = fsc[fs_rot["i"] % 2]
            fs_rot["i"] += 1
            qk_norm(zb, 0, 4, ssk, msk, rsk, fsq)
            pk = banks[zb][:, 0:256].rearrange("p (h d) -> p h d", h=4)
            kn4 = buf[:].rearrange("p (h a d) -> p h a d", h=4, a=2)
            op("dve", lambda e: e.tensor_tensor(
                out=kn4, in0=pk.unsqueeze(2).broadcast_to([128, 4, 2, 64]),
                in1=rsk.unsqueeze(2).unsqueeze(3).broadcast_to([128, 4, 2, 64]), op=ALU.mult),
               reads=bk(zb) + [("sm", id(rsk))], writes=[bkey])
            kr3 = rope_t[:, 256:320].rearrange("p (h f) -> p h f", h=4)
            op("dve", lambda e: e.tensor_tensor(
                out=kr3, in0=pk[:, :, 0:16], in1=rsk.unsqueeze(2).broadcast_to([128, 4, 16]), op=ALU.mult),
               reads=bk(zb) + [("sm", id(rsk))], writes=["ropesrc"])
            kro = rope_t[:, 320:384].rearrange("p (h f) -> p h f", h=4)
            rope(kr3, 4, "k", b, lambda lo: kro[:, :, lo * 8:(lo + 1) * 8], rope_t, ["ropedst"])
            op("dve", lambda e: e.tensor_copy(
                out=kn4[:, :, :, 0:16], in_=kro.unsqueeze(2).broadcast_to([128, 4, 2, 16])),
               reads=["ropedst"], writes=[bkey])
            op("act", lambda e: e.activation(
                out=vr[ring][:].rearrange("p (h d) -> p h d", h=4)[:, :, 0:64],
                in_=banks[zb][:, 256:512].rearrange("p (h d) -> p h d", h=4), func=AF.Copy),
               reads=bk(zb), writes=[("vr", ring)])

            def X():
                def ktr(e):
                    r = None
                    for h in range(4):
                        r = e.transpose(out=bankb[TB][:, h * 128:(h + 1) * 128], in_=buf[:, h * 128:(h + 1) * 128],
                                        identity=ident[:])
                    return r
                op("pe", ktr, reads=[bkey, "const"], writes=bk(TB))
                op("act", lambda e: e.activation(out=kTr[ring][:], in_=bankb[TB][:, 0:512],
                                                 func=AF.Identity, scale=kvec[:, 0:1]),
                   reads=bk(TB) + ["kvec"], writes=[("kTr", ring)])
            return X

        def q_M(s, t, qc, slot):
            b = s * NT + t
            gi = gidx["i"] % 4
            gidx["i"] += 1
            buf = knq[gi]
            bkey = ("knq", gi)
            zb = zbank()
            mm_tok(zb, hT3, t, slot, hkeys(t))
            fsq = fsc[fs_rot["i"] % 2]
            fs_rot["i"] += 1
            rq = rsq[qc]
            qk_norm(zb, 0, 8, ssq[qc], msq[qc], rq, fsq)
            pq = banks[zb][:, 0:512].rearrange("p (h d) -> p h d", h=8)
            qn3 = buf[:].rearrange("p (h d) -> p h d", h=8)
            op("dve", lambda e: e.tensor_tensor(
                out=qn3, in0=pq, in1=rq.unsqueeze(2).broadcast_to([128, 8, 64]), op=ALU.mult),
               reads=bk(zb) + [("sm", id(rq))], writes=[bkey])
            qr3 = rope_t[:, 384:512].rearrange("p (h f) -> p h f", h=8)
            op("dve", lambda e: e.tensor_tensor(
                out=qr3, in0=pq[:, :, 0:16], in1=rq.unsqueeze(2).broadcast_to([128, 8, 16]), op=ALU.mult),
               reads=bk(zb) + [("sm", id(rq))], writes=["ropesrc"])
            rope(qr3, 8, "q", b, lambda lo: qn3[:, :, lo * 8:(lo + 1) * 8], rope_t, [bkey])

            def X():
                def qtr(e):
                    r = None
                    for k in range(4):
                        r = e.transpose(out=bankb[TB][:, k * 128:(k + 1) * 128],
                                        in_=buf[:, k * 128:(k + 1) * 128], identity=ident[:])
                    return r
                op("pe", qtr, reads=[bkey, "const"], writes=bk(TB))
                op("act", lambda e: e.activation(
                    out=qT[t][:, qc * 512:(qc + 1) * 512], in_=bankb[TB][:, 0:512], func=AF.Identity,
                    scale=qvec[:, 0:1]), reads=bk(TB) + ["qvec"], writes=[("qT", t, qc)])
            return X

        def g_M(t, gc, slot):
            zb = zbank()
            mm_tok(zb, hT3, t, slot, hkeys(t))
            th = fsc[2 + fs_rot["i"] % 2]
            fs_rot["i"] += 1
            op("act", lambda e: e.activation(out=th[:, 0:512], in_=banks[zb][:, 0:512],
                                             func=AF.Tanh, scale=0.5),
               reads=bk(zb), writes=[("fs", id(th))])
            op("dve", lambda e: e.scalar_tensor_tensor(
                out=sgl[t][:, gc * 512:(gc + 1) * 512], in0=th[:, 0:512], scalar=1.0, in1=banks[zb][:, 0:512],
                op0=ALU.add, op1=ALU.mult), reads=bk(zb) + [("fs", id(th))], writes=[("sgl", t, gc)])

        def att_sc(s, t, h):
            b = s * NT + t
            rc, rp = b % 5, (b - 1) % 5
            first = (b == 0)

            def sc(e):
                r = None
                for kb, ring in ((0, rp), (1, rc)):
                    if first and kb == 0:
                        continue
                    for half, bank in ((0, 5), (1, 6)):
                        r0 = half * 64
                        r = e.matmul(banks[bank][:, kb * 256:(kb + 1) * 256],
                                     lhsT=kTr[ring][r0:r0 + 64, h * 128:(h + 1) * 128],
                                     rhs=qT[t][r0:r0 + 64, h * 256:(h + 1) * 256], start=True, stop=True)
                return r
            rd = [("kTr", rc), ("qT", t, h // 2)] + ([] if first else [("kTr", rp)])
            op("pe", sc, reads=rd, writes=bk(5) + bk(6))
            c0 = 256 if first else 0
            for half, bank in ((0, 5), (1, 6)):
                pb_i = (2 * h + half) % 4
                pbuf = Pb[pb_i]
                op("act", lambda e, pbuf=pbuf, bank=bank: e.activation(
                    out=pbuf[:, c0:512], in_=banks[bank][:, c0:512], func=AF.Exp, scale=0.125),
                   reads=bk(bank), writes=[("Pb", pb_i)])
                op("dve", lambda e, pbuf=pbuf: e.tensor_tensor(
                    out=pbuf[:, c0:512], in0=pbuf[:, c0:512], in1=maskt[:, c0:512], op=ALU.mult),
                   reads=[("Pb", pb_i), "const"], writes=[("Pb", pb_i)])

        def att_pv(s, t, h):
            b = s * NT + t
            rc, rp = b % 5, (b - 1) % 5
            first = (b == 0)
            mt = mixtok[t % 2]
            pbs = [Pb[(2 * h) % 4], Pb[(2 * h + 1) % 4]]

            def pv(e):
                r = None
                for j in range(2):
                    for half in range(2):
                        hl = 2 * j + half
                        for kb, ring in ((0, rp), (1, rc)):
                            if first and kb == 0:
                                continue
                            r = e.matmul(banks[7][:, hl * 65:(hl + 1) * 65],
                                         lhsT=pbs[half][:, kb * 256 + j * 128: kb * 256 + (j + 1) * 128],
                                         rhs=vr[ring][:, h * 65:(h + 1) * 65],
                                         start=(kb == 0 or first), stop=(kb == 1))
                return r
            rd = [("Pb", (2 * h) % 4), ("Pb", (2 * h + 1) % 4), ("vr", rc)] + ([] if first else [("vr", rp)])
            op("pe", pv, reads=rd, writes=bk(7))
            po3 = banks[7][:, 0:260].rearrange("p (h d) -> p h d", h=4)
            op("dve", lambda e: e.scalar_tensor_tensor(
                out=den, in0=po3[:, :, 64], scalar=2.0, in1=es2[:, 4 * h:4 * h + 4], op0=ALU.mult,
                op1=ALU.add), reads=bk(7) + ["es2"], writes=["den"])
            op("dve", lambda e: e.reciprocal(out=rden, in_=den), reads=["den"], writes=["rden"])
            tmp = fsc[h % 2]
            op("dve", lambda e: e.tensor_tensor(
                out=tmp[:, 0:256].rearrange("p (h d) -> p h d", h=4), in0=po3[:, :, 0:64],
                in1=rden.unsqueeze(2).broadcast_to([128, 4, 64]), op=ALU.mult),
               reads=bk(7) + ["rden"], writes=[("fs", id(tmp))])
            op("pool", lambda e: e.tensor_tensor(
                out=mt[:, h * 256:(h + 1) * 256], in0=tmp[:, 0:256], in1=sgl[t][:, h * 256:(h + 1) * 256],
                op=ALU.mult), reads=[("fs", id(tmp)), ("sgl", t, h // 2)], writes=[("mixtok", t % 2)])

        def att_fin(t):
            mt = mixtok[t % 2]

            def mtr(e):
                r = None
                for k in range(8):
                    r = e.transpose(out=bankb[TB][:, k * 128:(k + 1) * 128], in_=mt[:, k * 128:(k + 1) * 128],
                                    identity=ident[:])
                return r
            op("pe", mtr, reads=[("mixtok", t % 2), "const"], writes=bk(TB))
            op("act", lambda e: e.activation(
                out=mixT3[:, 0:8, t * 128:(t + 1) * 128],
                in_=bankb[TB][:, 0:1024].rearrange("p (c k) -> p c k", c=8), func=AF.Copy),
               reads=bk(TB), writes=[("mixT", "a", t)])

        def conv_seg(slot, si, bank):
            def mmseg(e):
                r = None
                for c in range(KC):
                    r = e.matmul(banks[bank][:, 0:512], lhsT=w3[slot][:, c, si * 128:(si + 1) * 128],
                                 rhs=hT3[:, c, 0:512], start=(c == 0), stop=(c == KC - 1))
                return r
            op("pe", mmseg, reads=hkeys_all + [("w", slot)], writes=bk(bank))

        def conv_u(j):
            ch = fsc[j % 2]
            u = fsc[4 + j % 2]
            op("act", lambda e: e.activation(out=ch[:, 0:512], in_=banks[1][:, 0:512], func=AF.Copy),
               reads=bk(1), writes=[("fs", id(ch))])
            op("dve", lambda e: e.tensor_copy(out=u[:, 0:2], in_=ucarry[:, 2 * j:2 * j + 2]),
               reads=["ucarry%d" % j], writes=[("fs", id(u))])
            op("dve", lambda e: e.tensor_tensor(out=u[:, 2:514], in0=banks[0][:, 0:512], in1=ch[:, 0:512],
                                                op=ALU.mult), reads=bk(0) + [("fs", id(ch))],
               writes=[("fs", id(u))])
            op("dve", lambda e: e.tensor_copy(out=ucarry[:, 2 * j:2 * j + 2], in_=u[:, 512:514]),
               reads=[("fs", id(u))], writes=["ucarry%d" % j])

        def conv_fin(j):
            u = fsc[4 + j % 2]
            acc = fsc[6 + j % 2]
            th = fsc[2 + j % 2]
            bcb, bg = 2, 3
            op("dve", lambda e: e.tensor_scalar(out=acc[:, 0:512], in0=u[:, 2:514], scalar1=cw[:, 3 * j + 2:3 * j + 3],
                                                 scalar2=None, op0=ALU.mult), reads=[("fs", id(u)), "cw"],
               writes=[("fs", id(acc))])
            op("dve", lambda e: e.scalar_tensor_tensor(out=acc[:, 0:512], in0=u[:, 1:513],
                                                        scalar=cw[:, 3 * j + 1:3 * j + 2], in1=acc[:, 0:512],
                                                        op0=ALU.mult, op1=ALU.add),
               reads=[("fs", id(u)), "cw", ("fs", id(acc))], writes=[("fs", id(acc))])
            op("dve", lambda e: e.scalar_tensor_tensor(out=acc[:, 0:512], in0=u[:, 0:512],
                                                        scalar=cw[:, 3 * j:3 * j + 1], in1=acc[:, 0:512],
                                                        op0=ALU.mult, op1=ALU.add),
               reads=[("fs", id(u)), "cw", ("fs", id(acc))], writes=[("fs", id(acc))])
            op("dve", lambda e: e.tensor_tensor(out=acc[:, 0:512], in0=acc[:, 0:512], in1=banks[bcb][:, 0:512],
                                                op=ALU.mult), reads=bk(bcb) + [("fs", id(acc))],
               writes=[("fs", id(acc))])
            op("act", lambda e: e.activation(out=th[:, 0:512], in_=banks[bg][:, 0:512], func=AF.Tanh, scale=0.5),
               reads=bk(bg), writes=[("fs", id(th))])
            op("dve", lambda e: e.scalar_tensor_tensor(out=th[:, 0:512], in0=th[:, 0:512], scalar=1.0,
                                                       in1=banks[bg][:, 0:512], op0=ALU.add, op1=ALU.mult),
               reads=bk(bg) + [("fs", id(th))], writes=[("fs", id(th))])
            op("dve", lambda e: e.tensor_tensor(out=mixT3[:, 8 + j, :], in0=acc[:, 0:512], in1=th[:, 0:512],
                                                op=ALU.mult), reads=[("fs", id(acc)), ("fs", id(th))],
               writes=[("mixT", "c", j)])

        def p_load(s, t):
            r0 = s * T + t * 128
            pi = t % 2
            op("pool", lambda e: [e.dma_start(out=psb[pi][:], in_=p_d[r0:r0 + 128, :])],
               writes=[("psb", pi)], dma_sem=p_sems[pi])
            op("act", lambda e: e.activation(out=pbf[t][:], in_=psb[pi][:], func=AF.Copy),
               reads=[("psb", pi)], writes=[("pbf", t)])

        def p_trans(t):
            def ptr(e):
                r = None
                for k in range(2):
                    r = e.transpose(out=bankb[6][:, k * 128:(k + 1) * 128], in_=pbf[t][:, k * 128:(k + 1) * 128],
                                    identity=ident[:])
                return r
            op("pe", ptr, reads=[("pbf", t), "const"], writes=bk(6))
            op("act", lambda e: e.activation(
                out=pT3[:, :, t * 128:(t + 1) * 128],
                in_=bankb[6][:, 0:256].rearrange("p (c k) -> p c k", c=2), func=AF.Copy),
               reads=bk(6), writes=[("pT", t)])

        def e1_stats(t):
            for n in range(4):
                zb = zbank()
                mm_tok(zb, pT3, t, None, [("pT", t)], kcs=2, w_ap3=wproj3[:, :, n * 512:(n + 1) * 512],
                       wkey="wproj")
                jk = fsc[fs_rot["i"] % 2]
                fs_rot["i"] += 1
                op("act", lambda e, jk=jk, zb=zb, n=n: e.activation(
                    out=jk[:, 0:512], in_=banks[zb][:, 0:512], func=AF.Square,
                    accum_out=ss3[:, t * 4 + n:t * 4 + n + 1]), reads=bk(zb),
                   writes=[("fs", id(jk)), ("ss3", t)])
            op("dve", lambda e: e.tensor_reduce(out=ss3t[:, t:t + 1], in_=ss3[:, t * 4:t * 4 + 4],
                                                axis=AX.X, op=ALU.add), reads=[("ss3", t)],
               writes=[("ss3t", t)])
            op("pool", lambda e: e.tensor_scalar(out=ms3[:, t:t + 1], in0=ss3t[:, t:t + 1],
                                                 scalar1=1.0 / D, scalar2=EPS, op0=ALU.mult, op1=ALU.add),
               reads=[("ss3t", t)], writes=[("ms3", t)])
            op("pool", lambda e: e.tensor_tensor(out=rs3[:, t:t + 1], in0=ms3[:, t:t + 1],
                                                 in1=neghalf[:, 0:1], op=ALU.pow),
               reads=[("ms3", t), "neghalf"], writes=[("rs3", t)])
            op("pool", lambda e: e.tensor_scalar(out=rs3[:, t:t + 1], in0=rs3[:, t:t + 1], scalar1=0.5,
                                                 scalar2=None, op0=ALU.mult),
               reads=[("rs3", t)], writes=[("rs3", t)])

        for s in range(NS):
            nA = lambda t: norm_part(xs[t], [("xs", t, n_) for n_ in range(4)], t, ss1[:, t:t + 1], ms1[:, t:t + 1], rs1[:, t:t + 1], "n1")
            if s == 0:
                nA(0)
                nA(1)
            pend = []
            LAG = 3
            slot = next_chunk()
            trans_part(0, g1T, hT3, "hT")
            nA(2)
            trans_part(1, g1T, hT3, "hT")
            nA(3)
            for t in range(NT):
                pend.append(kv_M(s, t, slot))
                if len(pend) > LAG:
                    pend.pop(0)()
                if t + 2 < NT:
                    trans_part(t + 2, g1T, hT3, "hT")
            for qc in range(2):
                slot = next_chunk()
                for t in range(NT):
                    pend.append(q_M(s, t, qc, slot))
                    if len(pend) > LAG:
                        pend.pop(0)()
            for gc in range(2):
                slot = next_chunk()
                for t in range(NT):
                    g_M(t, gc, slot)
                    if pend:
                        pend.pop(0)()
            while pend:
                pend.pop(0)()

            for t in range(NT):
                p_load(s, t)
            for j in range(8):
                slot = next_chunk()
                t = j // 2
                h0 = 2 * (j % 2)
                conv_seg(slot, 0, 0)
                if j % 2 == 0 and j > 0:
                    att_fin(t - 1)
                att_sc(s, t, h0)
                conv_seg(slot, 1, 1)
                conv_u(j)
                att_pv(s, t, h0)
                att_sc(s, t, h0 + 1)
                conv_seg(slot, 2, 2)
                conv_seg(slot, 3, 3)
                att_pv(s, t, h0 + 1)
                conv_fin(j)
            att_fin(NT - 1)

            mkeys_all = [("mixT", "c", j) for j in range(8)]
            for n in range(4):
                slot = next_chunk()
                for t in range(NT):
                    zb = zbank()
                    mm_tok(zb, mixT3, t, slot, [("mixT", "a", t)] + mkeys_all)
                    op("dve", lambda e, t=t, n=n, zb=zb: e.tensor_tensor(
                        out=xs[t][:, n * 512:(n + 1) * 512], in0=xs[t][:, n * 512:(n + 1) * 512],
                        in1=banks[zb][:, 0:512], op=ALU.add), reads=bk(zb) + [("xs", t, n)], writes=[("xs", t, n)])
                    if n == 0:
                        p_trans(t)
                    if n == 3 and t < 2:
                        pass

            if s == 0:
                issue_casts(17)
                op("sp", lambda e: [e.dma_start(out=wproj3, in_=wprojb_d.rearrange("(c p) n -> p c n", p=128))],
                   reads=[cast_keys[("proj", 0)]], writes=["wproj"], dma_sem=s_wproj)
            nD = lambda t: norm_part(xs[t], [("xs", t, n_) for n_ in range(4)], t, ss2[:, t:t + 1], ms2[:, t:t + 1], rs2[:, t:t + 1], "n2")
            nD(0)
            nD(1)
            e1_stats(0)
            e1_stats(1)
            trans_part(0, g2T, hT3, "hT")
            nD(2)
            e1_stats(2)
            trans_part(1, g2T, hT3, "hT")
            nD(3)
            e1_stats(3)
            trans_part(2, g2T, hT3, "hT")
            trans_part(3, g2T, hT3, "hT")

            def bias_load(n):
                bi = n % 2
                op("sp", lambda e: [
                    e.dma_start(out=bch[bi][:], in_=bias_d[:, n * 512:(n + 1) * 512]),
                    e.dma_start(out=pch[bi][:], in_=pg_d[:, n * 512:(n + 1) * 512])],
                   writes=[("bp", bi)], dma_sem=bp_sems[bi], n_dma=2)
            bias_load(0)
            for n in range(4):
                slot = next_chunk()
                bi = n % 2
                if n + 1 < 4:
                    bias_load(n + 1)
                for t in range(NT):
                    zg = zbank()
                    mm_tok(zg, hT3, t, slot, hkeys(t))
                    ze = zbank()
                    mm_tok(ze, pT3, t, None, [("pT", t)], kcs=2, w_ap3=wproj3[:, :, n * 512:(n + 1) * 512],
                           wkey="wproj")
                    gp = fsc[fs_rot["i"] % 2]
                    ge = fsc[4 + fs_rot["i"] % 2]
                    fs_rot["i"] += 1
                    op("dve", lambda e, gp=gp, zg=zg, bi=bi: e.tensor_tensor(
                        out=gp[:, 0:512], in0=banks[zg][:, 0:512], in1=bch[bi][:], op=ALU.add),
                       reads=bk(zg) + [("bp", bi)], writes=[("fs", id(gp))])
                    op("act", lambda e, gp=gp: e.activation(out=gp[:, 0:512], in_=gp[:, 0:512], func=AF.Tanh, scale=0.5),
                       reads=[("fs", id(gp))], writes=[("fs", id(gp))])
                    op("dve", lambda e, gp=gp, ge=ge, ze=ze: e.scalar_tensor_tensor(
                        out=ge[:, 0:512], in0=gp[:, 0:512], scalar=1.0, in1=banks[ze][:, 0:512], op0=ALU.add,
                        op1=ALU.mult), reads=bk(ze) + [("fs", id(gp))], writes=[("fs", id(ge))])
                    op("dve", lambda e, ge=ge, bi=bi: e.tensor_tensor(
                        out=ge[:, 0:512], in0=ge[:, 0:512], in1=pch[bi][:], op=ALU.mult),
                       reads=[("fs", id(ge)), ("bp", bi)], writes=[("fs", id(ge))])
                    op("dve", lambda e, ge=ge, t=t, n=n: e.scalar_tensor_tensor(
                        out=xs[t][:, n * 512:(n + 1) * 512], in0=ge[:, 0:512], scalar=rs3[:, t:t + 1],
                        in1=xs[t][:, n * 512:(n + 1) * 512], op0=ALU.mult, op1=ALU.add),
                       reads=[("fs", id(ge)), ("rs3", t), ("xs", t, n)], writes=[("xs", t, n)])
                    r0 = s * T + t * 128

                    def st(e, t=t, r0=r0, n=n):
                        return [e.dma_start(out=y_d[r0:r0 + 128, n * 512:(n + 1) * 512],
                                            in_=xs[t][:, n * 512:(n + 1) * 512])]
                    store_ops.append(op("sp", st, reads=[("xs", t, n)], dma_sem=y_sems[t], n_dma=1))
                    if n == 3:
                        if s + 1 < NS:
                            load_x(s + 1, t)
                            nAn = lambda tt: norm_part(xs[tt], [("xs", tt, n_) for n_ in range(4)], tt, ss1[:, tt:tt + 1], ms1[:, tt:tt + 1],
                                                       rs1[:, tt:tt + 1], "n1")
                            if t == 2:
                                nAn(0)
                            if t == 3:
                                nAn(1)
        fin = op("pool", None)
        fin.deps = list(store_ops[-4 * NT:])
        for o in fin.deps:
            o.has_dep = True

        with nc.Block() as block:
            Sd.emit(block, qs, dict(pe="tensor", act="scalar", dve="vector", pool="gpsimd", sp="sync"))
    return nc


_CACHE = {}


def _consts():
    half = 8
    inv_freq = np.power(np.float32(500000.0), -np.arange(half, dtype=np.float32) * np.float32(2.0) / np.float32(16))
    pos = np.arange(S, dtype=np.float32)
    ang = pos[:, None] * inv_freq[None, :]
    cos = np.cos(ang).astype(np.float32)
    sin = np.sin(ang).astype(np.float32)
    cost = np.ascontiguousarray(cos.reshape(32, 128, 8).transpose(1, 0, 2)).reshape(128, 256)
    sint = np.ascontiguousarray(sin.reshape(32, 128, 8).transpose(1, 0, 2)).reshape(128, 256)
    ident = np.eye(128, dtype=np.float32).astype(ml_dtypes.bfloat16)
    j = np.arange(128)[:, None]
    i = np.arange(128)[None, :]
    prev = (i < j).astype(np.float32)
    cur = (i >= j).astype(np.float32)
    mask = np.concatenate([prev, prev, cur, cur], axis=1).astype(ml_dtypes.bfloat16)
    rotm = ((np.arange(128) % 64) < 16).astype(np.float32).reshape(128, 1)
    return cost, sint, ident, mask, rotm


def kernel(x, p, norm_gain, w_in, q_norm_gain, k_norm_gain, attn_sinks, conv_w, w_out,
           ple_gate_norm_gain, w_ple_gate, b_ple_gate, w_ple_proj, ple_norm_gain):
    f = lambda a: np.ascontiguousarray(np.asarray(a, dtype=np.float32))
    x = f(x); p = f(p)
    cost, sint, ident, mask, rotm = _consts()
    ng = f(norm_gain)[0]
    g2 = f(ple_gate_norm_gain)[0]
    gq = f(q_norm_gain)[0]
    gk = f(k_norm_gain)[0]
    cwv = f(conv_w)[0]
    shared = dict(
        w_in=f(w_in)[0], w_out=f(w_out)[0], w_gate=f(w_ple_gate)[0], w_proj=f(w_ple_proj)[0],
        g1T=np.ascontiguousarray(ng.reshape(KC, 128).T), g2T=np.ascontiguousarray(g2.reshape(KC, 128).T),
        gqb=np.ascontiguousarray(np.broadcast_to(gq[None, :], (128, 64))),
        gkb=np.ascontiguousarray(np.broadcast_to(gk[None, :], (128, 64))),
        gqc=np.ascontiguousarray(np.tile(gq, 2).reshape(128, 1)),
        gkc=np.ascontiguousarray(np.tile(gk, 2).reshape(128, 1)),
        rotm=rotm,
        sinkb=np.ascontiguousarray(np.broadcast_to(f(attn_sinks)[0][None, :], (128, 16))),
        cw=np.ascontiguousarray(cwv.reshape(3, 8, 128).transpose(2, 1, 0)).reshape(128, 24),
        biasb=np.ascontiguousarray(np.broadcast_to(f(b_ple_gate)[0][None, :], (128, D))),
        pgb=np.ascontiguousarray(np.broadcast_to(f(ple_norm_gain)[0][None, :], (128, D))),
        cost=cost, sint=sint, ident=ident, maskt=mask,
    )
    if "nc" not in _CACHE:
        _CACHE["nc"] = build_program()
    nc = _CACHE["nc"]
    in_maps = []
    for c in range(N_CORES):
        m = dict(shared)
        m["x"] = x[c]
        m["p"] = p[0, c]
        in_maps.append(m)
    res = run_bass_kernel_spmd(nc, in_maps, core_ids=list(range(N_CORES)))
    out = np.stack([np.asarray(r["y"], dtype=np.float32) for r in res.results], axis=0)
    return out
```
